# Optimizing a Trainium2 kernel written in Bass

```python
import jax, jax.numpy as jnp
from jax import lax
import numpy as np

D_MODEL = 1024
BATCH = 4
SEQ = 8192
DEPTH = 1
DEC_BATCH = 16
DEC_SEQ = 32
PAST_LEN = 4096

CHUNK = 64
HEAD_DIM = 64
N_HEADS_A = 8
N_HEADS_B = 8
IDX_HEADS = 8
IDX_DIM = 64
TOPK_MAX = 256
D_FF = 4 * D_MODEL
ROPE_THETA = 10000.0
EPS = 1e-6
SB_Q_BLOCK = 128
DSA_Q_BLOCK = 64
WIDTH_A = N_HEADS_A * HEAD_DIM
WIDTH_B = N_HEADS_B * HEAD_DIM
SPLIT_SIZES = (WIDTH_A, WIDTH_A, WIDTH_A, IDX_HEADS * IDX_DIM, IDX_DIM, IDX_HEADS,
               WIDTH_B, WIDTH_B, WIDTH_B, D_MODEL, D_MODEL)
SPLIT_OFFSETS = tuple(int(v) for v in np.cumsum(SPLIT_SIZES)[:-1])
D_IN = int(sum(SPLIT_SIZES))

kernel_name = "hybrid_dsa_stickbreaking_stream_step"


def rmsnorm(x, g):
    xf = x.astype(jnp.float32)
    y = xf * lax.rsqrt(jnp.mean(xf * xf, axis=-1, keepdims=True) + EPS)
    return (y * g.astype(jnp.float32)).astype(x.dtype)


def rope(x, pos):
    half = x.shape[-1] // 2
    inv_freq = jnp.power(ROPE_THETA, -jnp.arange(half, dtype=jnp.float32) / half)
    ang = pos.astype(jnp.float32)[:, None] * inv_freq[None, :]
    cos = jnp.cos(ang)[None, :, None, :]
    sin = jnp.sin(ang)[None, :, None, :]
    xf = x.astype(jnp.float32)
    x1, x2 = xf[..., :half], xf[..., half:]
    return jnp.concatenate([x1 * cos - x2 * sin, x2 * cos + x1 * sin], axis=-1).astype(x.dtype)


def project_inputs(x, pos, g_mix, w_in, g_qn, g_kn):
    B, L, _ = x.shape
    h = rmsnorm(x, g_mix)
    z = jnp.einsum('bld,de->ble', h, w_in)
    q_a, k_a, v_a, q_i, k_i, w_i, q_b, k_b, v_b, gate_a, gate_b = jnp.split(z, SPLIT_OFFSETS, axis=-1)
    q_a = rope(rmsnorm(q_a.reshape(B, L, N_HEADS_A, HEAD_DIM), g_qn), pos)
    k_a = rope(rmsnorm(k_a.reshape(B, L, N_HEADS_A, HEAD_DIM), g_kn), pos)
    v_a = v_a.reshape(B, L, N_HEADS_A, HEAD_DIM)
    q_i = rope(q_i.reshape(B, L, IDX_HEADS, IDX_DIM), pos)
    k_i = rope(k_i[:, :, None, :], pos)[:, :, 0, :]
    w_i = w_i * ((IDX_HEADS ** -0.5) * (IDX_DIM ** -0.5))
    q_b = q_b.reshape(B, L, N_HEADS_B, HEAD_DIM)
    k_b = k_b.reshape(B, L, N_HEADS_B, HEAD_DIM)
    v_b = v_b.reshape(B, L, N_HEADS_B, HEAD_DIM)
    return q_a, k_a, v_a, q_i, k_i, w_i, q_b, k_b, v_b, gate_a, gate_b


def gather_rows(a, idx):
    return jax.vmap(lambda ab, ib: ab[ib])(a, idx)


def dsa_block(q, q_i, w_i, qpos, k, v, k_i, topk):
    L = k.shape[1]
    kpos = jnp.arange(L, dtype=jnp.int32)
    s_idx = jnp.einsum('bqhd,bld->bqhl', q_i, k_i).astype(jnp.float32)
    score = jnp.einsum('bqhl,bqh->bql', jax.nn.relu(s_idx), w_i.astype(jnp.float32))
    admissible = (kpos // CHUNK)[None, :] <= (qpos // CHUNK)[:, None]
    score = jnp.where(admissible[None], score, -jnp.inf)
    _, idx = lax.top_k(score, topk)
    valid = (idx // CHUNK) <= (qpos // CHUNK)[None, :, None]
    kg = gather_rows(k, idx)
    vg = gather_rows(v, idx)
    logits = jnp.einsum('bqhd,bqkhd->bqhk', q, kg).astype(jnp.float32) * (HEAD_DIM ** -0.5)
    logits = jnp.where(valid[:, :, None, :], logits, -jnp.inf)
    p = jax.nn.softmax(logits, axis=-1)
    return jnp.einsum('bqhk,bqkhd->bqhd', p.astype(v.dtype), vg)


def stick_breaking_block(q, qpos, k, v):
    L = k.shape[1]
    kpos = jnp.arange(L, dtype=jnp.int32)
    z = jnp.einsum('bqhd,blhd->bhql', q, k).astype(jnp.float32) * (HEAD_DIM ** -0.5)
    causal = (kpos[None, :] < qpos[:, None])[None, None]
    sp = jnp.where(causal, jax.nn.softplus(z), 0.0)
    tail = lax.cumsum(sp, axis=3, reverse=True) - sp
    a = jnp.where(causal, jnp.exp(jax.nn.log_sigmoid(z) - tail), 0.0)
    return jnp.einsum('bhql,blhd->bqhd', a.astype(v.dtype), v)


def to_blocks(a, size):
    B, S = a.shape[:2]
    return jnp.moveaxis(a.reshape((B, S // size, size) + a.shape[2:]), 1, 0)


def from_blocks(a):
    nb, B, size = a.shape[:3]
    return jnp.moveaxis(a, 0, 1).reshape((B, nb * size) + a.shape[3:])


def merge_and_ffn(x, o_a, o_b, gate_a, gate_b, w_branch_a, w_branch_b, w_out, g_ffn, w_up, w_down):
    B, L = x.shape[:2]
    pa = jnp.einsum('blc,cd->bld', o_a.reshape(B, L, WIDTH_A), w_branch_a)
    pb = jnp.einsum('blc,cd->bld', o_b.reshape(B, L, WIDTH_B), w_branch_b)
    m = jax.nn.sigmoid(gate_a) * pa + jax.nn.sigmoid(gate_b) * pb
    h = x + jnp.einsum('bld,de->ble', m, w_out)
    u = jnp.einsum('bld,df->blf', rmsnorm(h, g_ffn), w_up)
    return h + jnp.einsum('blf,fd->bld', jnp.square(jax.nn.relu(u)), w_down)


def setup_inputs(seed: int = 0) -> dict:
    key = jax.random.key(seed)
    ks = jax.random.split(key, 18)
    nrm = jax.random.normal
    f32 = jnp.float32
    return {
        "x_prompt": nrm(ks[0], (BATCH, SEQ, D_MODEL), f32),
        "x_sample": nrm(ks[1], (DEC_BATCH, DEC_SEQ, D_MODEL), f32),
        "cache_k_a": nrm(ks[2], (DEC_BATCH, PAST_LEN, N_HEADS_A, HEAD_DIM), f32),
        "cache_v_a": nrm(ks[3], (DEC_BATCH, PAST_LEN, N_HEADS_A, HEAD_DIM), f32),
        "cache_k_idx": nrm(ks[4], (DEC_BATCH, PAST_LEN, IDX_DIM), f32),
        "cache_k_sb": nrm(ks[5], (DEC_BATCH, PAST_LEN, N_HEADS_B, HEAD_DIM), f32),
        "cache_v_sb": nrm(ks[6], (DEC_BATCH, PAST_LEN, N_HEADS_B, HEAD_DIM), f32),
        "g_mix": 1.0 + 0.01 * nrm(ks[7], (D_MODEL,), f32),
        "w_in": nrm(ks[8], (D_MODEL, D_IN), f32) * D_MODEL ** -0.5,
        "g_qn": 1.0 + 0.01 * nrm(ks[9], (HEAD_DIM,), f32),
        "g_kn": 1.0 + 0.01 * nrm(ks[10], (HEAD_DIM,), f32),
        "w_branch_a": nrm(ks[11], (WIDTH_A, D_MODEL), f32) * WIDTH_A ** -0.5,
        "w_branch_b": nrm(ks[12], (WIDTH_B, D_MODEL), f32) * WIDTH_B ** -0.5,
        "w_out": nrm(ks[13], (D_MODEL, D_MODEL), f32) * D_MODEL ** -0.5,
        "g_ffn": 1.0 + 0.01 * nrm(ks[14], (D_MODEL,), f32),
        "w_up": nrm(ks[15], (D_MODEL, D_FF), f32) * D_MODEL ** -0.5,
        "w_down": nrm(ks[16], (D_FF, D_MODEL), f32) * D_FF ** -0.5,
    }


def reference(x_prompt, x_sample, cache_k_a, cache_v_a, cache_k_idx, cache_k_sb, cache_v_sb,
              g_mix, w_in, g_qn, g_kn, w_branch_a, w_branch_b, w_out, g_ffn, w_up, w_down):
    S = x_prompt.shape[1]
    T = x_sample.shape[1]
    past = cache_k_a.shape[1]

    pos_p = jnp.arange(S, dtype=jnp.int32)
    xp = x_prompt
    for _ in range(DEPTH):
        q_a, k_a_p, v_a_p, q_i, k_idx_p, w_i, q_b, k_sb_p, v_sb_p, ga, gb = project_inputs(
            xp, pos_p, g_mix, w_in, g_qn, g_kn)
        topk_p = min(TOPK_MAX, S // 4)
        o_a = from_blocks(lax.map(
            lambda blk: dsa_block(blk[0], blk[1], blk[2], blk[3], k_a_p, v_a_p, k_idx_p, topk_p),
            (to_blocks(q_a, DSA_Q_BLOCK), to_blocks(q_i, DSA_Q_BLOCK), to_blocks(w_i, DSA_Q_BLOCK),
             pos_p.reshape(-1, DSA_Q_BLOCK))))
        o_b = from_blocks(lax.map(
            lambda blk: stick_breaking_block(blk[0], blk[1], k_sb_p, v_sb_p),
            (to_blocks(q_b, SB_Q_BLOCK), pos_p.reshape(-1, SB_Q_BLOCK))))
        xp = merge_and_ffn(xp, o_a, o_b, ga, gb, w_branch_a, w_branch_b, w_out, g_ffn, w_up, w_down)
    y_prompt = xp

    pos_s = past + jnp.arange(T, dtype=jnp.int32)
    xs = x_sample
    for _ in range(DEPTH):
        q_a_s, k_a_s, v_a_s, q_i_s, k_idx_s, w_i_s, q_b_s, k_sb_s, v_sb_s, ga_s, gb_s = project_inputs(
            xs, pos_s, g_mix, w_in, g_qn, g_kn)
        ka_all = jnp.concatenate([cache_k_a, k_a_s], axis=1)
        va_all = jnp.concatenate([cache_v_a, v_a_s], axis=1)
        ki_all = jnp.concatenate([cache_k_idx, k_idx_s], axis=1)
        kb_all = jnp.concatenate([cache_k_sb, k_sb_s], axis=1)
        vb_all = jnp.concatenate([cache_v_sb, v_sb_s], axis=1)
        topk_s = min(TOPK_MAX, (past + T) // 4)
        o_a_s = dsa_block(q_a_s, q_i_s, w_i_s, pos_s, ka_all, va_all, ki_all, topk_s)
        o_b_s = stick_breaking_block(q_b_s, pos_s, kb_all, vb_all)
        xs = merge_and_ffn(xs, o_a_s, o_b_s, ga_s, gb_s, w_branch_a, w_branch_b, w_out, g_ffn, w_up, w_down)
    y_sample = xs

    return (y_prompt, y_sample, k_a_p, v_a_p, k_idx_p, k_sb_p, v_sb_p,
            k_a_s, v_a_s, k_idx_s, k_sb_s, v_sb_s)
```

```python
import numpy as np
import ml_dtypes
from contextlib import ExitStack
import concourse.bass as bass
import concourse.mybir as mybir
from concourse.bass_utils import run_bass_kernel_spmd

F32 = mybir.dt.float32
BF16 = mybir.dt.bfloat16
AF = mybir.ActivationFunctionType
ALU = mybir.AluOpType
AX = mybir.AxisListType
NPBF = ml_dtypes.bfloat16


class Buf:
    __slots__ = ("t", "lw", "rd", "name")

    def __init__(self, t, name=""):
        self.t = t
        self.lw = None
        self.rd = {}
        self.name = name

    def __getitem__(self, k):
        return self.t[k]


class FW:
    NDMA = 40

    def __init__(self, nc, es):
        self.nc = nc
        self.es = es
        self.eng = {"pe": nc.tensor, "act": nc.scalar, "dve": nc.vector, "pool": nc.gpsimd, "sp": nc.sync}
        self.sems = {}
        self.cnt = {}
        for k in self.eng:
            self.sems[k] = es.enter_context(nc.semaphore("s_" + k))
            self.cnt[k] = 0
        self.dsem = {"sp": [], "pool": []}
        for q, pre, n in (("sp", "d", 24), ("pool", "g", 20)):
            for i in range(n):
                k = "%s%d" % (pre, i)
                self.sems[k] = es.enter_context(nc.semaphore("s_" + k))
                self.cnt[k] = 0
                self.dsem[q].append(k)
        self.dnext = {"sp": 0, "pool": 0}
        self.cur = {k: k for k in self.sems}
        self.epoch = {}
        self.seen = {e: {} for e in self.eng}
        self.ninst = 0
        self.per = {}

    def sb(self, name, shape, dt, es=None):
        t = (es or self.es).enter_context(self.nc.sbuf_tensor(name, list(shape), dt))
        return Buf(t, name)

    def ps(self, name, shape, dt, es=None):
        t = (es or self.es).enter_context(self.nc.psum_tensor(name, list(shape), dt))
        return Buf(t, name)

    def _wait(self, e, tok):
        k, v = tok
        if self.seen[e].get(k, 0) >= v:
            return
        self.eng[e].wait_ge(self.sems[k], v)
        self.seen[e][k] = v
        self.ninst += 1
        self.per[e] = self.per.get(e, 0) + 1

    def _deps(self, e, reads, writes):
        for b in reads:
            if b.lw is not None and not (e == "pe" and b.lw[0].startswith("pe")):
                self._wait(e, b.lw)
        for b in writes:
            if b.lw is not None and not (e == "pe" and b.lw[0].startswith("pe")):
                self._wait(e, b.lw)
            for k, v in b.rd.items():
                if e == "pe" and k.startswith("pe"):
                    continue
                self._wait(e, (k, v))

    def _commit(self, tok, reads, writes):
        for b in writes:
            b.lw = tok
            b.rd = {}
        for b in reads:
            if b.rd.get(tok[0], 0) < tok[1]:
                b.rd[tok[0]] = tok[1]

    LIMIT = 24000

    def _roll(self, lk):
        self.epoch[lk] = self.epoch.get(lk, 0) + 1
        k = "%s#%d" % (lk, self.epoch[lk])
        self.sems[k] = self.es.enter_context(self.nc.semaphore("s_" + k.replace("#", "_")))
        self.cnt[k] = 0
        self.cur[lk] = k
        return k

    def op(self, e, fn, reads=(), writes=()):
        self._deps(e, reads, writes)
        k = self.cur[e]
        if self.cnt[k] >= self.LIMIT:
            k = self._roll(e)
        ins = fn(self.eng[e])
        self.cnt[k] += 1
        ins.then_inc(self.sems[k], 1)
        self._commit((k, self.cnt[k]), reads, writes)
        self.ninst += 1
        self.per[e] = self.per.get(e, 0) + 1
        return ins

    def dma(self, q, out, in_, reads=(), writes=()):
        self._deps(q, reads, writes)
        lk = self.dsem[q][self.dnext[q]]
        self.dnext[q] = (self.dnext[q] + 1) % len(self.dsem[q])
        k = self.cur[lk]
        if self.cnt[k] > 0:
            self._wait(q, (k, self.cnt[k]))
        if self.cnt[k] >= self.LIMIT:
            k = self._roll(lk)
        ins = self.eng[q].dma_start(out=out, in_=in_)
        self.cnt[k] += 16
        ins.then_inc(self.sems[k], 16)
        self._commit((k, self.cnt[k]), reads, writes)
        self.ninst += 1
        return ins

    def barrier(self):
        for e in self.eng:
            for k in list(self.sems):
                if k.split("#")[0] != e and self.cnt[k] > 0:
                    self._wait(e, (k, self.cnt[k]))

    def finish(self, q="sp"):
        for k in list(self.sems):
            if k.split("#")[0] != q and self.cnt[k] > 0:
                self._wait(q, (k, self.cnt[k]))


class Rot:
    def __init__(self, fw, name, shape, dt, n, es=None, psum=False, init=None):
        mk = fw.ps if psum else fw.sb
        self.bufs = [mk("%s%d" % (name, i), shape, dt, es) for i in range(n)]
        self.i = 0
        if init is not None:
            for b in self.bufs:
                init(b)

    def next(self):
        b = self.bufs[self.i]
        self.i = (self.i + 1) % len(self.bufs)
        return b


D = 1024
DFF = 4096
NH = 8
HD = 64
W = NH * HD
DIN = 5704
C_QA, C_KA, C_VA, C_QI, C_KI, C_WI, C_QB, C_KB, C_VB, C_GA, C_GB = 0, 512, 1024, 1536, 2048, 2112, 2120, 2632, 3144, 3656, 4680
NAB = 3656
EPS = 1e-6
NEG = -30000.0
MASKED = -1.0e30
WI_SCALE = float((8 ** -0.5) * (64 ** -0.5))
NBIS = 14
CNT_DVE_FRAC = 0.42
NDUMMY_D = 0
DEBUG = False
LAST = {}


def build_program(S, PAST, KP, KS):
    NTP = S // 128
    NSP = NTP // 2
    NTC = PAST // 128
    NTS = NTC + 1
    NOWN = NSP + 2
    LMAX = max(NTP, NTS) * 128

    nc = bass.Bass("TRN2", target_bir_lowering=False)

    def din(name, shape, dt=F32):
        return nc.dram_tensor(name, list(shape), dt, kind="ExternalInput").ap()

    def dout(name, shape, dt=F32):
        return nc.dram_tensor(name, list(shape), dt, kind="ExternalOutput").ap()

    def dscr(name, shape, dt=BF16):
        return nc.dram_tensor(name, list(shape), dt, kind="Internal").ap()

    x_all = din("x_all", [NTP, 128, D])
    x_own = din("x_own", [NOWN, 128, D])
    rope_all = din("rope_all", [NTP, 128, 64])
    rope_own = din("rope_own", [NOWN, 128, 64])
    ck_a = din("ck_a", [2, NTC, 128, W]); cv_a = din("cv_a", [2, NTC, 128, W])
    ck_i = din("ck_i", [2, NTC, 128, 64])
    ck_b = din("ck_b", [2, NTC, 128, W]); cv_b = din("cv_b", [2, NTC, 128, W])
    w_in = din("w_in", [D, DIN]); g_mix = din("g_mix", [128, 8])
    g_qn = din("g_qn", [64]); g_kn = din("g_kn", [64])
    w_ba = din("w_ba", [W, D]); w_bb = din("w_bb", [W, D]); w_out = din("w_out", [D, D])
    g_ffn = din("g_ffn", [128, 8]); w_up = din("w_up", [D, DFF]); w_down = din("w_down", [DFF, D])
    c_ident = din("c_ident", [128, 128], BF16); c_ident2 = din("c_ident2", [128, 256], BF16)
    c_tri = din("c_tri", [128, 128], BF16); c_ones = din("c_ones", [128, 128], BF16)
    c_dmask = din("c_dmask", [2, 128, 256])
    c_sbm = din("c_sbm", [2, 2, 128, 256], BF16)

    y_own = dout("y_own", [NOWN, 128, D])
    o_ka = dout("o_ka", [NTP, 128, W]); o_va = dout("o_va", [NTP, 128, W]); o_ki = dout("o_ki", [NTP, 128, 64])
    o_kb = dout("o_kb", [NTP, 128, W]); o_vb = dout("o_vb", [NTP, 128, W])
    s_ka = dout("s_ka", [2, 128, W]); s_va = dout("s_va", [2, 128, W]); s_ki = dout("s_ki", [2, 128, 64])
    s_kb = dout("s_kb", [2, 128, W]); s_vb = dout("s_vb", [2, 128, W])

    class Seq:
        pass

    seqs = []
    for si in range(3):
        q = Seq()
        q.idx = si
        q.nt = NTP if si == 0 else NTS
        q.kaT = dscr("kaT%d" % si, [4, 128, q.nt * 128]); q.kbT = dscr("kbT%d" % si, [4, 128, q.nt * 128])
        q.kiT = dscr("kiT%d" % si, [128, q.nt * 128])
        q.va = dscr("va%d" % si, [q.nt, 128, 520]); q.vb = dscr("vb%d" % si, [q.nt, 128, W])
        q.b_kaT = Buf(q.kaT); q.b_kbT = Buf(q.kbT); q.b_kiT = Buf(q.kiT); q.b_va = Buf(q.va); q.b_vb = Buf(q.vb)
        if si == 0:
            q.slots = [(i, 2 * i + 2) for i in range(NSP)]
            q.K = KP
            q.mi = 0
        else:
            q.slots = [(NSP + si - 1, NTS)]
            q.K = KS
            q.mi = 1
        seqs.append(q)
    qaT_d = dscr("qaT_d", [NOWN, 128, 4, 256]); qbT_d = dscr("qbT_d", [NOWN, 128, 4, 256])
    qiT_d = dscr("qiT_d", [NOWN, 128, 4, 128]); wi_d = dscr("wi_d", [NOWN, 128, 8], F32)
    negm_d = (dout if DEBUG else dscr)("negm_d", [NOWN, 128, LMAX], BF16)
    dbg = dout if DEBUG else dscr
    oaT_d = dbg("oaT_d", [NOWN, 128, 4, 128], BF16); obT_d = dbg("obT_d", [NOWN, 128, 4, 128], BF16)
    h_d = dbg("h_d", [NOWN, 128, D], F32)
    b_q = [Buf(None) for _ in range(NOWN)]
    b_negm = [Buf(None) for _ in range(NOWN)]
    b_oa = [Buf(None) for _ in range(NOWN)]
    b_ob = [Buf(None) for _ in range(NOWN)]
    b_h = [Buf(None) for _ in range(NOWN)]

    es = ExitStack()
    with es:
        fw = FW(nc, es)
        op, dma = fw.op, fw.dma
        ident = fw.sb("ident", [128, 128], BF16); ident2 = fw.sb("ident2", [128, 256], BF16)
        tri = fw.sb("tri", [128, 128], BF16); ones = fw.sb("ones", [128, 128], BF16)
        dmask = fw.sb("dmask", [128, 2, 256], F32); sbm = fw.sb("sbm", [128, 4, 256], BF16)
        gq = fw.sb("gq", [128, 64], F32); gk = fw.sb("gk", [128, 64], F32)
        gmix = fw.sb("gmix", [128, 8], F32); gffn = fw.sb("gffn", [128, 8], F32)
        dma("sp", ident[:], c_ident[:, :], writes=[ident]); dma("sp", ident2[:], c_ident2[:, :], writes=[ident2])
        dma("sp", tri[:], c_tri[:, :], writes=[tri]); dma("sp", ones[:], c_ones[:, :], writes=[ones])
        dma("sp", dmask[:], c_dmask.rearrange("a p n -> p a n"), writes=[dmask])
        dma("sp", sbm[:], c_sbm.rearrange("a b p n -> p (a b) n"), writes=[sbm])
        dma("sp", gq[:], g_qn.partition_broadcast(128), writes=[gq]); dma("sp", gk[:], g_kn.partition_broadcast(128), writes=[gk])
        dma("sp", gmix[:], g_mix[:, :], writes=[gmix]); dma("sp", gffn[:], g_ffn[:, :], writes=[gffn])
        PT = Rot(fw, "PT", [128, 8, 128], BF16, 2, psum=True)
        PB = [fw.ps("PB%d" % i, [128, 512], F32) for i in range(6)]

        class RR:
            def __init__(self, bufs):
                self.bufs = bufs; self.i = 0

            def next(self):
                b = self.bufs[self.i]; self.i = (self.i + 1) % len(self.bufs); return b

        def load_w(dst, src, C, n, gvec, stage, coloff=0):
            for c in range(C):
                for lo in range(0, n, 1024):
                    m = min(1024, n - lo)
                    st = stage.next()
                    dma("sp", st[:, :m], src[c * 128:(c + 1) * 128, lo:lo + m], writes=[st])
                    if gvec is not None:
                        op("act", lambda e: e.activation(out=dst[:, c, coloff + lo:coloff + lo + m], in_=st[:, :m], func=AF.Copy,
                                                         scale=gvec[:, c:c + 1]), [st, gvec], [dst])
                    else:
                        op("dve", lambda e: e.tensor_copy(out=dst[:, c, coloff + lo:coloff + lo + m], in_=st[:, :m]), [st], [dst])

        with ExitStack() as pes:
            wab = fw.sb("wab", [128, 8, NAB], BF16, pes)
            stage = Rot(fw, "stg", [128, 1024], F32, 2, pes)
            load_w(wab, w_in[:, 0:NAB], 8, NAB, gmix, stage)
            R_xs = Rot(fw, "xs", [128, D], F32, 4, pes)
            R_xb = Rot(fw, "xb", [128, D], BF16, 4, pes)
            R_xT = Rot(fw, "xT", [128, 8, 128], BF16, 4, pes)
            junk = fw.sb("junk", [128, D], BF16, pes)
            R_ssq = Rot(fw, "ssq", [128, 1], F32, 2, pes)
            R_rstd = Rot(fw, "rstd", [128, 1], F32, 5, pes)
            R_rope = Rot(fw, "rope", [128, 64], F32, 5, pes)
            R_z = Rot(fw, "z", [128, W], F32, 10, pes)
            R_zo = Rot(fw, "zo", [128, W], F32, 8, pes)
            R_zb = Rot(fw, "zb", [128, W], BF16, 10, pes)
            R_t = Rot(fw, "tt", [128, 256], F32, 12, pes)
            R_sq = Rot(fw, "sqh", [128, W], F32, 2, pes)
            R_s8 = Rot(fw, "s8", [128, 8], F32, 4, pes)
            R_kT = Rot(fw, "kT", [128, 4, 128], BF16, 5, pes)
            R_kiT = Rot(fw, "kiTt", [64, 128], BF16, 2, pes)
            R_wi = Rot(fw, "wit", [128, 8], F32, 2, pes)
            R_P = RR(PB)

            def zero_init(b):
                op("pool", lambda e: e.memset(b[:], 0.0), [], [b])

            def ones_init(b):
                op("pool", lambda e: e.memset(b[:], 1.0), [], [b])

            R_qbd = Rot(fw, "qbd", [128, 4, 256], BF16, 3, pes, init=zero_init)
            R_vat = Rot(fw, "vat", [128, 8, 65], BF16, 5, pes, init=ones_init)
            R_vbt = Rot(fw, "vbt", [128, W], BF16, 5, pes)

            def front(x_ap, rope_ap):
                xs = R_xs.next()
                dma("sp", xs[:], x_ap, writes=[xs])
                rp = R_rope.next()
                dma("sp", rp[:], rope_ap, writes=[rp])
                ssq = R_ssq.next()
                op("act", lambda e: e.activation(out=junk[:], in_=xs[:], func=AF.Square, accum_out=ssq[:]), [xs], [junk, ssq])
                rstd = R_rstd.next()
                op("act", lambda e: e.activation(out=rstd[:], in_=ssq[:], func=AF.Sqrt, scale=1.0 / D, bias=EPS), [ssq], [rstd])
                op("dve", lambda e: e.reciprocal(out=rstd[:], in_=rstd[:]), [rstd], [rstd])
                xb = R_xb.next()
                op("act", lambda e: e.copy(out=xb[:], in_=xs[:]), [xs], [xb])
                pt = PT.bufs[0]
                for c in range(8):
                    op("pe", lambda e: e.transpose(out=pt[:, c, :], in_=xb[:, c * 128:(c + 1) * 128], identity=ident[:]), [xb, ident], [pt])
                xT = R_xT.next()
                op("act", lambda e: e.copy(out=xT[:], in_=pt[:]), [pt], [xT])
                return xs, xT, rstd, rp

            def proj(xT, lo, n):
                P = R_P.next()
                for c in range(8):
                    op("pe", lambda e: e.matmul(P[:, :n], lhsT=xT[:, c, :], rhs=wab[:, c, lo:lo + n], start=(c == 0), stop=(c == 7)),
                       [xT, wab], [P])
                return P

            def evac(P, n, rstd):
                z = R_z.next()
                op("act", lambda e: e.activation(out=z[:, :n], in_=P[:, :n], func=AF.Copy, scale=rstd[:, 0:1]), [P, rstd], [z])
                return z

            def v3(ap, h):
                return ap.rearrange("p (h d) -> p h d", h=h)

            def headnorm(z, g):
                sq = R_sq.next()
                op("act", lambda e: e.activation(out=sq[:], in_=z[:], func=AF.Square), [z], [sq])
                s8 = R_s8.next()
                op("dve", lambda e: e.tensor_reduce(out=s8[:], in_=v3(sq[:], 8), axis=AX.X, op=ALU.add), [sq], [s8])
                op("act", lambda e: e.activation(out=s8[:], in_=s8[:], func=AF.Sqrt, scale=1.0 / HD, bias=EPS), [s8], [s8])
                op("dve", lambda e: e.reciprocal(out=s8[:], in_=s8[:]), [s8], [s8])
                zn = R_zo.next()
                op("dve", lambda e: e.tensor_tensor(out=v3(zn[:], 8), in0=v3(z[:], 8), in1=s8[:].unsqueeze(2).to_broadcast([128, 8, HD]),
                                                    op=ALU.mult), [z, s8], [zn])
                op("dve", lambda e: e.tensor_tensor(out=v3(zn[:], 8), in0=v3(zn[:], 8), in1=g[:].unsqueeze(1).to_broadcast([128, 8, HD]),
                                                    op=ALU.mult), [zn, g], [zn])
                return zn

            def rope(z, rp, H):
                n = H * HD
                o = R_zo.next()
                zv = z[:, :n].rearrange("p (h t d) -> p h t d", h=H, t=2)
                ov = o[:, :n].rearrange("p (h t d) -> p h t d", h=H, t=2)
                cos = rp[:, 0:32].unsqueeze(1).to_broadcast([128, H, 32])
                sin = rp[:, 32:64].unsqueeze(1).to_broadcast([128, H, 32])
                t1, t2, t3, t4 = R_t.next(), R_t.next(), R_t.next(), R_t.next()

                def tv(t):
                    return t[:, :H * 32].rearrange("p (h d) -> p h d", h=H)
                e1 = "dve"
                e2 = "pool" if H == 1 else "dve"
                op(e1, lambda e: e.tensor_tensor(out=tv(t1), in0=zv[:, :, 0, :], in1=cos, op=ALU.mult), [z, rp], [t1])
                op(e2, lambda e: e.tensor_tensor(out=tv(t2), in0=zv[:, :, 1, :], in1=sin, op=ALU.mult), [z, rp], [t2])
                op(e1, lambda e: e.tensor_tensor(out=ov[:, :, 0, :], in0=tv(t1), in1=tv(t2), op=ALU.subtract), [t1, t2], [o])
                op(e2, lambda e: e.tensor_tensor(out=tv(t3), in0=zv[:, :, 1, :], in1=cos, op=ALU.mult), [z, rp], [t3])
                op(e1, lambda e: e.tensor_tensor(out=tv(t4), in0=zv[:, :, 0, :], in1=sin, op=ALU.mult), [z, rp], [t4])
                op(e2, lambda e: e.tensor_tensor(out=ov[:, :, 1, :], in0=tv(t3), in1=tv(t4), op=ALU.add), [t3, t4], [o])
                return o

            def to_T(z, n):
                zb = R_zb.next()
                op("act", lambda e: e.copy(out=zb[:, :n], in_=z[:, :n]), [z], [zb])
                pt = PT.next()
                if n == 64:
                    op("pe", lambda e: e.transpose(out=pt[0:64, 0, :], in_=zb[:, 0:64], identity=ident[:]), [zb, ident], [pt])
                else:
                    for j in range(n // 128):
                        op("pe", lambda e: e.transpose(out=pt[:, j, :], in_=zb[:, j * 128:(j + 1) * 128], identity=ident[:]), [zb, ident], [pt])
                return pt

            def emit_kT(z, dstT, bdst, t):
                pt = to_T(z, W)
                kT = R_kT.next()
                op("dve", lambda e: e.tensor_copy(out=kT[:], in_=pt[:, 0:4, :]), [pt], [kT])
                dma("pool", dstT.rearrange("a p n -> p a n")[:, :, t * 128:(t + 1) * 128], kT[:], reads=[kT], writes=[bdst])

            def emit_kiT(z, seq, t):
                pt = to_T(z, 64)
                kt = R_kiT.next()
                op("act", lambda e: e.copy(out=kt[:], in_=pt[0:64, 0, :]), [pt], [kt])
                dma("pool", seq.kiT[0:64, t * 128:(t + 1) * 128], kt[:], reads=[kt], writes=[seq.b_kiT])
                dma("pool", seq.kiT[64:128, t * 128:(t + 1) * 128], kt[:], reads=[kt], writes=[seq.b_kiT])

            def emit_va(z, seq, t):
                vt = R_vat.next()
                op("dve", lambda e: e.tensor_copy(out=vt[:, :, 0:64], in_=v3(z[:], 8)), [z], [vt])
                dma("pool", seq.va[t], vt[:].rearrange("p h d -> p (h d)"), reads=[vt], writes=[seq.b_va])

            def emit_vb(z, seq, t):
                vt = R_vbt.next()
                op("dve", lambda e: e.tensor_copy(out=vt[:], in_=z[:]), [z], [vt])
                dma("pool", seq.vb[t], vt[:], reads=[vt], writes=[seq.b_vb])

            def emit_qbd(z, dst, own):
                pt = to_T(z, W)
                qb = R_qbd.next()
                op("act", lambda e: e.copy(out=qb[0:64, :, 0:128], in_=pt[0:64, 0:4, :]), [pt], [qb])
                op("dve", lambda e: e.tensor_copy(out=qb[64:128, :, 128:256], in_=pt[64:128, 0:4, :]), [pt], [qb])
                dma("pool", dst[own], qb[:], reads=[qb], writes=[b_q[own]])

            R_kT8 = Rot(fw, "kT8", [128, 8, 128], BF16, 3, pes)

            def cast_bf(z, n):
                zb = R_zb.next()
                op("dve", lambda e: e.tensor_copy(out=zb[:, :n], in_=z[:, :n]), [z], [zb])
                return zb

            def proj_multi(xT, specs):
                Ps = [R_P.next() for _ in specs]
                for c in range(8):
                    for P, (lo, n) in zip(Ps, specs):
                        op("pe", lambda e: e.matmul(P[:, :n], lhsT=xT[:, c, :], rhs=wab[:, c, lo:lo + n], start=(c == 0), stop=(c == 7)),
                           [xT, wab], [P])
                return Ps

            def kside_proj(xT):
                return proj_multi(xT, [(C_KA, W), (C_VA, W), (C_KI, 64)]) + proj_multi(xT, [(C_KB, W), (C_VB, W)])

            def kside_post(seq, t, Ps, rstd, rp, outs):
                oka, ova, oki, okb, ovb = outs
                z_ka = evac(Ps[0], W, rstd); z_va = evac(Ps[1], W, rstd); z_ki = evac(Ps[2], 64, rstd)
                z_kb = evac(Ps[3], W, rstd); z_vb = evac(Ps[4], W, rstd)
                o_ka = rope(headnorm(z_ka, gk), rp, 8)
                dma("pool", oka, o_ka[:], reads=[o_ka])
                zb_ka = cast_bf(o_ka, W)
                dma("pool", okb, z_kb[:], reads=[z_kb])
                zb_kb = cast_bf(z_kb, W)
                o_ki = rope(z_ki, rp, 1)
                dma("pool", oki, o_ki[:, 0:64], reads=[o_ki])
                zb_ki = cast_bf(o_ki, 64)
                dma("pool", ova, z_va[:], reads=[z_va])
                emit_va(z_va, seq, t)
                dma("pool", ovb, z_vb[:], reads=[z_vb])
                emit_vb(z_vb, seq, t)
                return (seq, t, zb_ka, zb_kb, zb_ki)

            def kside_T(seq, t, zb_ka, zb_kb, zb_ki):
                pt = PT.bufs[1]
                for j in range(4):
                    op("pe", lambda e: e.transpose(out=pt[:, j, :], in_=zb_ka[:, j * 128:(j + 1) * 128], identity=ident[:]), [zb_ka, ident], [pt])
                for j in range(4):
                    op("pe", lambda e: e.transpose(out=pt[:, 4 + j, :], in_=zb_kb[:, j * 128:(j + 1) * 128], identity=ident[:]), [zb_kb, ident], [pt])
                k8 = R_kT8.next()
                op("act", lambda e: e.copy(out=k8[:], in_=pt[:]), [pt], [k8])
                dma("pool", seq.kaT.rearrange("a p n -> p a n")[:, :, t * 128:(t + 1) * 128], k8[:, 0:4, :], reads=[k8], writes=[seq.b_kaT])
                dma("pool", seq.kbT.rearrange("a p n -> p a n")[:, :, t * 128:(t + 1) * 128], k8[:, 4:8, :], reads=[k8], writes=[seq.b_kbT])
                pt2 = PT.bufs[1]
                op("pe", lambda e: e.transpose(out=pt2[0:64, 0, :], in_=zb_ki[:, 0:64], identity=ident[:]), [zb_ki, ident], [pt2])
                kt = R_kiT.next()
                op("act", lambda e: e.copy(out=kt[:], in_=pt2[0:64, 0, :]), [pt2], [kt])
                dma("pool", seq.kiT[0:64, t * 128:(t + 1) * 128], kt[:], reads=[kt], writes=[seq.b_kiT])
                dma("pool", seq.kiT[64:128, t * 128:(t + 1) * 128], kt[:], reads=[kt], writes=[seq.b_kiT])

            def qside_proj(xT):
                return proj_multi(xT, [(C_QA, W), (C_QI, W)]) + proj_multi(xT, [(C_WI, 8), (C_QB, W)])

            def qside_post(own, Ps, rstd, rp):
                z_qa = evac(Ps[0], W, rstd); z_qi = evac(Ps[1], W, rstd)
                wt = R_wi.next()
                op("dve", lambda e: e.tensor_scalar(out=wt[:], in0=Ps[2][:, 0:8], scalar1=rstd[:, 0:1], scalar2=WI_SCALE, op0=ALU.mult, op1=ALU.mult),
                   [Ps[2], rstd], [wt])
                z_qb = evac(Ps[3], W, rstd)
                dma("pool", wi_d[own], wt[:], reads=[wt], writes=[b_q[own]])
                zb_qb = cast_bf(z_qb, W)
                o_qa = rope(headnorm(z_qa, gq), rp, 8)
                zb_qa = cast_bf(o_qa, W)
                o_qi = rope(z_qi, rp, 8)
                zb_qi = cast_bf(o_qi, W)
                return (own, zb_qa, zb_qb, zb_qi)

            def qside_T(own, zb_qa, zb_qb, zb_qi):
                pt = PT.bufs[1]
                for j in range(4):
                    op("pe", lambda e: e.transpose(out=pt[:, j, :], in_=zb_qa[:, j * 128:(j + 1) * 128], identity=ident[:]), [zb_qa, ident], [pt])
                for j in range(4):
                    op("pe", lambda e: e.transpose(out=pt[:, 4 + j, :], in_=zb_qb[:, j * 128:(j + 1) * 128], identity=ident[:]), [zb_qb, ident], [pt])
                for (o4, dst) in ((0, qaT_d), (4, qbT_d)):
                    qb = R_qbd.next()
                    op("act", lambda e: e.copy(out=qb[0:64, :, 0:128], in_=pt[0:64, o4:o4 + 4, :]), [pt], [qb])
                    op("act", lambda e: e.copy(out=qb[64:128, :, 128:256], in_=pt[64:128, o4:o4 + 4, :]), [pt], [qb])
                    dma("pool", dst[own], qb[:], reads=[qb], writes=[b_q[own]])
                pt2 = PT.bufs[1]
                for j in range(4):
                    op("pe", lambda e: e.transpose(out=pt2[:, j, :], in_=zb_qi[:, j * 128:(j + 1) * 128], identity=ident[:]), [zb_qi, ident], [pt2])
                kT = R_kT.next()
                op("act", lambda e: e.copy(out=kT[:], in_=pt2[:, 0:4, :]), [pt2], [kT])
                dma("pool", qiT_d[own], kT[:], reads=[kT], writes=[b_q[own]])

            sq0 = seqs[0]
            pend = None
            frq = [front(x_all[u], rope_all[u]) for u in range(min(2, NTP))]
            for t in range(NTP):
                xs, xT, rstd, rp = frq.pop(0)
                Ps = kside_proj(xT)
                if t + 2 < NTP:
                    frq.append(front(x_all[t + 2], rope_all[t + 2]))
                if pend is not None:
                    kside_T(*pend)
                pend = kside_post(sq0, t, Ps, rstd, rp, (o_ka[t], o_va[t], o_ki[t], o_kb[t], o_vb[t]))
            kside_T(*pend)
            pend = None
            frq = [front(x_own[u], rope_own[u]) for u in range(min(2, NSP))]
            for i in range(NSP):
                xs, xT, rstd, rp = frq.pop(0)
                Ps = qside_proj(xT)
                if i + 2 < NSP:
                    frq.append(front(x_own[i + 2], rope_own[i + 2]))
                if pend is not None:
                    qside_T(*pend)
                pend = qside_post(i, Ps, rstd, rp)
            qside_T(*pend)
            for s in range(2):
                sq = seqs[1 + s]
                pend = None
                for t in range(NTC):
                    z = R_z.next(); dma("sp", z[:], ck_a[s, t], writes=[z]); zb_ka = cast_bf(z, W)
                    z = R_z.next(); dma("sp", z[:], ck_b[s, t], writes=[z]); zb_kb = cast_bf(z, W)
                    z = R_z.next(); dma("sp", z[:, 0:64], ck_i[s, t], writes=[z]); zb_ki = cast_bf(z, 64)
                    z = R_z.next(); dma("sp", z[:], cv_a[s, t], writes=[z]); emit_va(z, sq, t)
                    z = R_z.next(); dma("sp", z[:], cv_b[s, t], writes=[z]); emit_vb(z, sq, t)
                    if pend is not None:
                        kside_T(*pend)
                    pend = (sq, t, zb_ka, zb_kb, zb_ki)
                kside_T(*pend)
                own = NSP + s
                xs, xT, rstd, rp = front(x_own[own], rope_own[own])
                Ps = kside_proj(xT)
                kside_T(*kside_post(sq, NTC, Ps, rstd, rp, (s_ka[s], s_va[s], s_ki[s], s_kb[s], s_vb[s])))
                Ps = qside_proj(xT)
                qside_T(*qside_post(own, Ps, rstd, rp))
            fw.barrier()

        NTM = max(NTP, NTS)
        NVC = 4
        VCH = (NTM + NVC - 1) // NVC

        def gen_c2(seq, own, nkb, prs, nm, kaT, vaR, acc, LG, R_q, R_p, R_oa, R_rd, R_oT, hbase):
            qa = R_q.next(); dma("sp", qa[:], qaT_d[own], reads=[b_q[own]], writes=[qa])
            op("dve", lambda e: e.memset(acc[:], 0.0), [], [acc])
            pend = []

            def stage1(pr, kb):
                lg = LG.next()
                op("pe", lambda e: e.matmul(lg[:, 0:256], lhsT=kaT[pr][:, kb * 128:(kb + 1) * 128], rhs=qa[:, pr, :], start=True, stop=False),
                   [kaT[pr], qa], [lg])
                op("pe", lambda e: e.matmul(lg[:, 0:256], lhsT=nm[:, kb * 128:(kb + 1) * 128], rhs=ident2[:], start=False, stop=True),
                   [nm, ident2], [lg])
                p = R_p.next()
                op("act", lambda e: e.activation(out=p[:], in_=lg[:, 0:256], func=AF.Exp, scale=0.125), [lg], [p])
                return p

            def stage2(pr, kb, p):
                vb_ = vaR[kb // VCH]
                kk = kb % VCH
                for hh in range(2):
                    hl = 2 * pr + hh - hbase
                    c = hl * 65
                    op("pe", lambda e: e.matmul(acc[:, c:c + 65], lhsT=p[:, hh * 128:(hh + 1) * 128], rhs=vb_[:, kk, hl * 65:(hl + 1) * 65],
                                                start=False, stop=False, skip_group_check=True), [p, vb_], [acc])

            for pr in prs:
                for kb in range(nkb):
                    p = stage1(pr, kb)
                    pend.append((pr, kb, p))
                    if len(pend) > 2:
                        stage2(*pend.pop(0))
                    yield
            while pend:
                stage2(*pend.pop(0))
            oa = R_oa.next()
            rd = R_rd.next()
            av = acc[:, 0:260].rearrange("p (h d) -> p h d", h=4)
            op("dve", lambda e: e.reciprocal(out=rd[:], in_=av[:, :, 64]), [acc], [rd])
            op("dve", lambda e: e.tensor_tensor(out=oa[:].rearrange("p (h d) -> p h d", h=4), in0=av[:, :, 0:64],
                                                in1=rd[:].unsqueeze(2).to_broadcast([128, 4, 64]), op=ALU.mult), [acc, rd], [oa])
            pt = PT.next()
            for j in range(2):
                op("pe", lambda e: e.transpose(out=pt[:, j, :], in_=oa[:, j * 128:(j + 1) * 128], identity=ident[:]), [oa, ident], [pt])
            oT = R_oT.next()
            op("act", lambda e: e.copy(out=oT[:], in_=pt[:, 0:2, :]), [pt], [oT])
            c0 = hbase // 2
            dma("pool", oaT_d[own][:, c0:c0 + 2, :], oT[:], reads=[oT], writes=[b_oa[own]])
            yield

        def interleave(g1, g2, n1, n2):
            d1 = d2 = 0
            a1 = a2 = True
            while a1 or a2:
                if a1 and (not a2 or d1 * n2 <= d2 * n1):
                    try:
                        next(g1); d1 += 1
                    except StopIteration:
                        a1 = False
                else:
                    try:
                        next(g2); d2 += 1
                    except StopIteration:
                        a2 = False

        with ExitStack() as pes:
            NCH = (LMAX + 2047) // 2048
            kiT = [fw.sb("kiTr%d" % j, [128, 2048], BF16, pes) for j in range(NCH)]
            sc = fw.sb("score", [128, LMAX], F32, pes)
            R_nm = Rot(fw, "negm", [128, LMAX], BF16, 2, pes)
            sjunk = fw.sb("sjunk", [128, LMAX // 2], BF16, pes)
            sjunk2 = fw.sb("sjunk2", [128, LMAX * 3 // 4 + 256], BF16, pes)
            R_r = Rot(fw, "relu", [128, 512], F32, 3, pes)
            R_qi = Rot(fw, "qiTs", [128, 4, 128], BF16, 2, pes)
            R_wi2 = Rot(fw, "wis", [128, 8], F32, 2, pes)
            R_sm = Rot(fw, "sm", [128, 1], F32, 24, pes)
            R_w0 = Rot(fw, "smw", [128, 1], F32, 4, pes)
            R_Pc = RR(PB[4:6])
            kaTA = {pr: fw.sb("kaTA%d" % pr, [128, LMAX], BF16, pes) for pr in (0, 1)}
            vaA = [fw.sb("vaA%d" % j, [128, VCH, 260], BF16, pes) for j in range(NVC)]
            R_qA = Rot(fw, "qaTsA", [128, 4, 256], BF16, 2, pes)
            R_pA = Rot(fw, "pTA", [128, 256], BF16, 6, pes)
            R_oaA = Rot(fw, "oaA", [128, 256], BF16, 2, pes)
            R_rdA = Rot(fw, "rdenA", [128, 4], F32, 2, pes)
            R_oTA = Rot(fw, "oaTA", [128, 2, 128], BF16, 2, pes)
            LGA = RR(PB[0:2]); ACCA = RR(PB[2:4])

            def gen_c1(seq, own, nkb, holder):
                L = nkb * 128
                qi = R_qi.next(); dma("sp", qi[:], qiT_d[own], reads=[b_q[own]], writes=[qi])
                wi = R_wi2.next(); dma("sp", wi[:], wi_d[own], reads=[b_q[own]], writes=[wi])
                for g0 in range(0, L, 512):
                    n = min(512, L - g0)
                    kb_ = kiT[g0 // 2048]
                    c0 = g0 % 2048
                    for h in range(8):
                        pb = (h % 2) * 64
                        P = R_Pc.next()
                        op("pe", lambda e: e.matmul(P[:, :n], lhsT=qi[pb:pb + 64, h // 2, :], rhs=kb_[pb:pb + 64, c0:c0 + n], start=True, stop=True),
                           [qi, kb_], [P])
                        r = R_r.next()
                        op("act", lambda e: e.activation(out=r[:, :n], in_=P[:, :n], func=AF.Relu), [P], [r])
                        if h == 0:
                            op("dve", lambda e: e.tensor_scalar(out=sc[:, g0:g0 + n], in0=r[:, :n], scalar1=wi[:, 0:1], scalar2=None, op0=ALU.mult),
                               [r, wi], [sc])
                        else:
                            op("dve", lambda e: e.scalar_tensor_tensor(out=sc[:, g0:g0 + n], in0=r[:, :n], scalar=wi[:, h:h + 1], in1=sc[:, g0:g0 + n],
                                                                       op0=ALU.mult, op1=ALU.add), [r, wi, sc], [sc])
                    yield
                op("dve", lambda e: e.tensor_tensor(out=sc[:, L - 256:L], in0=sc[:, L - 256:L], in1=dmask[:, seq.mi, :], op=ALU.add), [sc, dmask], [sc])
                lo = R_sm.next()
                if L - 256 < seq.K:
                    op("dve", lambda e: e.memset(lo[:], -1.0e29), [], [lo])
                else:
                    hi = R_w0.next(); w0 = R_w0.next()
                    op("dve", lambda e: e.tensor_reduce(out=hi[:], in_=sc[:, :L], axis=AX.X, op=ALU.max), [sc], [hi])
                    op("dve", lambda e: e.tensor_reduce(out=lo[:], in_=sc[:, :L - 256], axis=AX.X, op=ALU.min), [sc], [lo])
                    op("dve", lambda e: e.tensor_tensor(out=w0[:], in0=hi[:], in1=lo[:], op=ALU.subtract), [hi, lo], [w0])
                    yield
                    Lh = max(128, int(L * CNT_DVE_FRAC) // 128 * 128)
                    nact = L - Lh
                    for it in range(NBIS):
                        f = 2.0 ** -(it + 1)
                        mid = R_sm.next(); cnt = R_sm.next(); sg_ = R_sm.next(); tt_ = R_sm.next(); pred = R_sm.next(); lo2 = R_sm.next()
                        op("dve", lambda e: e.scalar_tensor_tensor(out=mid[:], in0=w0[:], scalar=f, in1=lo[:], op0=ALU.mult, op1=ALU.add),
                           [w0, lo], [mid])
                        op("act", lambda e: e.activation(out=sjunk2[:, :nact], in_=sc[:, Lh:L], func=AF.Sign, scale=-1.0, bias=mid[:, 0:1], accum_out=sg_[:]),
                           [sc, mid], [sjunk2, sg_])
                        op("dve", lambda e: e.tensor_scalar(out=sjunk[:, :Lh], in0=sc[:, :Lh], scalar1=mid[:, 0:1], scalar2=0.0, op0=ALU.is_ge, op1=ALU.add,
                                                            accum_out=cnt[:]), [sc, mid], [sjunk, cnt])
                        op("dve", lambda e: e.scalar_tensor_tensor(out=tt_[:], in0=sg_[:], scalar=-0.5, in1=cnt[:], op0=ALU.mult, op1=ALU.add),
                           [sg_, cnt], [tt_])
                        op("dve", lambda e: e.tensor_scalar(out=pred[:], in0=tt_[:], scalar1=float(seq.K) - nact / 2.0, scalar2=f, op0=ALU.is_ge, op1=ALU.mult),
                           [tt_], [pred])
                        op("dve", lambda e: e.scalar_tensor_tensor(out=lo2[:], in0=pred[:], scalar=w0[:, 0:1], in1=lo[:], op0=ALU.mult, op1=ALU.add),
                           [pred, w0, lo], [lo2])
                        lo = lo2
                        yield
                nm = R_nm.next()
                op("dve", lambda e: e.tensor_scalar(out=nm[:, :L], in0=sc[:, :L], scalar1=lo[:, 0:1], scalar2=NEG, op0=ALU.is_lt, op1=ALU.mult),
                   [sc, lo], [nm])
                dma("pool", negm_d[own, :, 0:L], nm[:, :L], reads=[nm], writes=[b_negm[own]])
                holder.append(nm)
                yield

            for seq in seqs:
                L_all = seq.nt * 128
                for j in range((L_all + 2047) // 2048):
                    m = min(2048, L_all - j * 2048)
                    dma("sp", kiT[j][:, :m], seq.kiT[:, j * 2048:j * 2048 + m], reads=[seq.b_kiT], writes=[kiT[j]])
                for pr in (0, 1):
                    dma("sp", kaTA[pr][:, :L_all], seq.kaT[pr], reads=[seq.b_kaT], writes=[kaTA[pr]])
                for j in range(NVC):
                    t0 = j * VCH
                    m = min(VCH, seq.nt - t0)
                    for u in range(0, max(m, 0), 8):
                        mm = min(8, m - u)
                        dma("sp", vaA[j][:, u:u + mm, :], seq.va[t0 + u:t0 + u + mm, :, 0:260].rearrange("t p n -> p t n"), reads=[seq.b_va], writes=[vaA[j]])
                masks = {}
                prev = None
                for (own, nkb) in seq.slots:
                    L = nkb * 128
                    hold = []
                    g1 = gen_c1(seq, own, nkb, hold)
                    n1 = (L + 511) // 512 + NBIS + 2
                    if prev is None:
                        for _ in g1:
                            pass
                    else:
                        pown, pnkb, pnm = prev
                        g2 = gen_c2(seq, pown, pnkb, (0, 1), pnm, kaTA, vaA, ACCA.next(), LGA, R_qA, R_pA, R_oaA, R_rdA, R_oTA, 0)
                        interleave(g1, g2, n1, 2 * pnkb + 1)
                    prev = (own, nkb, hold[0])
                pown, pnkb, pnm = prev
                for _ in gen_c2(seq, pown, pnkb, (0, 1), pnm, kaTA, vaA, ACCA.next(), LGA, R_qA, R_pA, R_oaA, R_rdA, R_oTA, 0):
                    pass
            fw.barrier()

        with ExitStack() as pes:
            kaTB = {pr: fw.sb("kaTB%d" % pr, [128, LMAX], BF16, pes) for pr in (2, 3)}
            vaB = [fw.sb("vaB%d" % j, [128, VCH, 260], BF16, pes) for j in range(NVC)]
            R_nm2 = Rot(fw, "negm2", [128, LMAX], BF16, 2, pes)
            R_qB = Rot(fw, "qaTsB", [128, 4, 256], BF16, 2, pes)
            R_pB = Rot(fw, "pTB", [128, 256], BF16, 6, pes)
            R_oaB = Rot(fw, "oaB", [128, 256], BF16, 2, pes)
            R_rdB = Rot(fw, "rdenB", [128, 4], F32, 2, pes)
            R_oTB = Rot(fw, "oaTB", [128, 2, 128], BF16, 2, pes)
            LGB = RR(PB[0:3]); ACCB = RR(PB[3:6])
            for seq in seqs:
                L_all = seq.nt * 128
                for pr in (2, 3):
                    dma("sp", kaTB[pr][:, :L_all], seq.kaT[pr], reads=[seq.b_kaT], writes=[kaTB[pr]])
                for j in range(NVC):
                    t0 = j * VCH
                    m = min(VCH, seq.nt - t0)
                    for u in range(0, max(m, 0), 8):
                        mm = min(8, m - u)
                        dma("sp", vaB[j][:, u:u + mm, :], seq.va[t0 + u:t0 + u + mm, :, 260:520].rearrange("t p n -> p t n"), reads=[seq.b_va], writes=[vaB[j]])
                for (own, nkb) in seq.slots:
                    L = nkb * 128
                    nm = R_nm2.next(); dma("sp", nm[:, :L], negm_d[own, :, 0:L], reads=[b_negm[own]], writes=[nm])
                    for _ in gen_c2(seq, own, nkb, (2, 3), nm, kaTB, vaB, ACCB.next(), LGB, R_qB, R_pB, R_oaB, R_rdB, R_oTB, 4):
                        pass
            fw.barrier()

        with ExitStack() as pes:
            kbT = [fw.sb("kbTr%d" % j, [128, LMAX], BF16, pes) for j in range(4)]
            vbR = [fw.sb("vbR%d" % j, [128, VCH, W], BF16, pes) for j in range(NVC)]
            R_q = Rot(fw, "qbTs", [128, 4, 256], BF16, 2, pes)
            R_e = Rot(fw, "sbe", [128, 512], F32, 4, pes)
            R_sp = Rot(fw, "sbsp", [128, 512], BF16, 4, pes)
            R_f = Rot(fw, "sbf", [128, 512], F32, 3, pes)
            R_a = Rot(fw, "sba", [128, 512], BF16, 4, pes)
            R_S = Rot(fw, "sbS", [128, 512], BF16, 4, pes)
            R_ob = Rot(fw, "ob", [128, W], BF16, 2, pes)
            R_oT = Rot(fw, "obT", [128, 4, 128], BF16, 2, pes)
            ZB = RR(PB[0:2]); XB = RR(PB[2:4]); ACC = RR(PB[4:6])
            ptd = PT.bufs[1]
            for seq in seqs:
                L_all = seq.nt * 128
                for j in range(4):
                    dma("sp", kbT[j][:, :L_all], seq.kbT[j], reads=[seq.b_kbT], writes=[kbT[j]])
                for j in range(NVC):
                    t0 = j * VCH
                    m = min(VCH, seq.nt - t0)
                    for u in range(0, max(m, 0), 8):
                        mm = min(8, m - u)
                        dma("sp", vbR[j][:, u:u + mm, :], seq.vb[t0 + u:t0 + u + mm].rearrange("t p n -> p t n"), reads=[seq.b_vb], writes=[vbR[j]])
                for (own, nkb) in seq.slots:
                    qb = R_q.next(); dma("sp", qb[:], qbT_d[own], reads=[b_q[own]], writes=[qb])
                    acc = ACC.next()
                    op("dve", lambda e: e.memset(acc[:], 0.0), [], [acc])
                    nst = (nkb + 1) // 2
                    for pr in range(4):
                        state = {"S": None}
                        pendB = []
                        pendC = []

                        def stA(m):
                            nb = 2 if 2 * m + 1 < nkb else 1
                            w = 256 * nb
                            kbs = [nkb - 1 - (2 * m + j) for j in range(nb)]
                            z = ZB.next()
                            for j, kb in enumerate(kbs):
                                op("pe", lambda e: e.matmul(z[:, j * 256:(j + 1) * 256], lhsT=kbT[pr][:, kb * 128:(kb + 1) * 128], rhs=qb[:, pr, :], start=True, stop=True),
                                   [kbT[pr], qb], [z])
                            ee = R_e.next()
                            op("act", lambda e: e.activation(out=ee[:, :w], in_=z[:, :w], func=AF.Exp, scale=0.125), [z], [ee])
                            if m == 0:
                                mk = sbm[:, seq.mi * 2:seq.mi * 2 + 2, :].rearrange("p a n -> p (a n)")
                                op("dve", lambda e: e.tensor_tensor(out=ee[:, :w], in0=ee[:, :w], in1=mk[:, :w], op=ALU.mult), [ee, sbm], [ee])
                            sp = R_sp.next()
                            op("act", lambda e: e.activation(out=sp[:, :w], in_=ee[:, :w], func=AF.Ln, bias=1.0), [ee], [sp])
                            Scur = state["S"]
                            if Scur is not None and nb == 2:
                                op("pool", lambda e: e.tensor_tensor(out=Scur[:, 256:512], in0=Scur[:, 0:256], in1=sp[:, 0:256], op=ALU.add), [Scur, sp], [Scur])
                            if m < nst - 1:
                                Sn = R_S.next()
                                if Scur is None:
                                    op("pool", lambda e: e.tensor_tensor(out=Sn[:, 0:256], in0=sp[:, 0:256], in1=sp[:, 256:512], op=ALU.add), [sp], [Sn])
                                else:
                                    op("pool", lambda e: e.tensor_tensor(out=Sn[:, 0:256], in0=Scur[:, 256:512], in1=sp[:, 256:512], op=ALU.add), [Scur, sp], [Sn])
                                state["S"] = Sn
                            return (m, nb, w, kbs, ee, sp, Scur, None)

                        def stB(m, nb, w, kbs, ee, sp, Scur, _unused):
                            x = XB.next()
                            op("pe", lambda e: e.matmul(x[:, :w], lhsT=tri[:], rhs=sp[:, :w], start=True, stop=False, skip_group_check=True), [tri, sp], [x])
                            if Scur is not None:
                                op("pe", lambda e: e.matmul(x[:, :w], lhsT=ones[:], rhs=Scur[:, :w], start=False, stop=True, skip_group_check=True), [ones, Scur], [x])
                            elif nb == 2:
                                op("pe", lambda e: e.matmul(x[:, 256:512], lhsT=ones[:], rhs=sp[:, 0:256], start=False, stop=True, skip_group_check=True), [ones, sp], [x])
                            f = R_f.next()
                            op("act", lambda e: e.activation(out=f[:, :w], in_=x[:, :w], func=AF.Exp, scale=-1.0), [x], [f])
                            a = R_a.next()
                            op("dve", lambda e: e.tensor_tensor(out=a[:, :w], in0=ee[:, :w], in1=f[:, :w], op=ALU.mult), [ee, f], [a])
                            return (kbs, a)

                        def stC(kbs, a):
                            for j, kb in enumerate(kbs):
                                vb_ = vbR[kb // VCH]
                                kk = kb % VCH
                                for hh in range(2):
                                    h = 2 * pr + hh
                                    c0 = j * 256 + hh * 128
                                    op("pe", lambda e: e.matmul(acc[:, h * 64:(h + 1) * 64], lhsT=a[:, c0:c0 + 128], rhs=vb_[:, kk, h * 64:(h + 1) * 64],
                                                                start=False, stop=False, skip_group_check=True), [a, vb_], [acc])

                        for m in range(nst):
                            pendB.append(stA(m))
                            if len(pendB) > 1:
                                pendC.append(stB(*pendB.pop(0)))
                            if len(pendC) > 1:
                                stC(*pendC.pop(0))
                        while pendB:
                            pendC.append(stB(*pendB.pop(0)))
                            if len(pendC) > 1:
                                stC(*pendC.pop(0))
                        while pendC:
                            stC(*pendC.pop(0))
                    ob = R_ob.next()
                    op("act", lambda e: e.copy(out=ob[:], in_=acc[:]), [acc], [ob])
                    pt = PT.next()
                    for j in range(4):
                        op("pe", lambda e: e.transpose(out=pt[:, j, :], in_=ob[:, j * 128:(j + 1) * 128], identity=ident[:]), [ob, ident], [pt])
                    oT = R_oT.next()
                    op("act", lambda e: e.copy(out=oT[:], in_=pt[:, 0:4, :]), [pt], [oT])
                    dma("pool", obT_d[own], oT[:], reads=[oT], writes=[b_ob[own]])
            fw.barrier()

        with ExitStack() as pes:
            wg = fw.sb("wg", [128, 8, 2048], BF16, pes)
            wba = fw.sb("wba", [128, 4, D], BF16, pes); wbb = fw.sb("wbb", [128, 4, D], BF16, pes)
            wo = fw.sb("wo", [128, 8, D], BF16, pes)
            stage = Rot(fw, "stgE", [128, 1024], F32, 2, pes)
            load_w(wg, w_in[:, C_GA:DIN], 8, 2048, gmix, stage)
            load_w(wba, w_ba, 4, D, None, stage); load_w(wbb, w_bb, 4, D, None, stage); load_w(wo, w_out, 8, D, None, stage)
            R_xs = Rot(fw, "xsE", [128, D], F32, 2, pes)
            R_xb = Rot(fw, "xbE", [128, D], BF16, 2, pes)
            R_xT = Rot(fw, "xTE", [128, 8, 128], BF16, 2, pes)
            junk = fw.sb("junkE", [128, D], BF16, pes)
            R_ssq = Rot(fw, "ssqE", [128, 1], F32, 2, pes)
            R_rstd = Rot(fw, "rstdE", [128, 1], F32, 3, pes)
            R_sg = Rot(fw, "sg", [128, 2048], F32, 2, pes)
            R_oaT = Rot(fw, "oaTE", [128, 4, 128], BF16, 2, pes); R_obT = Rot(fw, "obTE", [128, 4, 128], BF16, 2, pes)
            R_t1 = Rot(fw, "t1E", [128, D], F32, 2, pes); R_t2 = Rot(fw, "t2E", [128, D], F32, 2, pes)
            R_m = Rot(fw, "mE", [128, D], BF16, 2, pes)
            R_mT = Rot(fw, "mTE", [128, 8, 128], BF16, 2, pes)
            R_h = Rot(fw, "hE", [128, D], F32, 2, pes)
            R_P = RR(PB)
            for own in range(NOWN):
                xs = R_xs.next(); dma("sp", xs[:], x_own[own], writes=[xs])
                ssq = R_ssq.next()
                op("act", lambda e: e.activation(out=junk[:], in_=xs[:], func=AF.Square, accum_out=ssq[:]), [xs], [junk, ssq])
                rstd = R_rstd.next()
                op("act", lambda e: e.activation(out=rstd[:], in_=ssq[:], func=AF.Sqrt, scale=1.0 / D, bias=EPS), [ssq], [rstd])
                op("dve", lambda e: e.reciprocal(out=rstd[:], in_=rstd[:]), [rstd], [rstd])
                xb = R_xb.next()
                op("dve", lambda e: e.tensor_copy(out=xb[:], in_=xs[:]), [xs], [xb])
                pt = PT.next()
                for c in range(8):
                    op("pe", lambda e: e.transpose(out=pt[:, c, :], in_=xb[:, c * 128:(c + 1) * 128], identity=ident[:]), [xb, ident], [pt])
                xT = R_xT.next()
                op("act", lambda e: e.copy(out=xT[:], in_=pt[:]), [pt], [xT])
                sg = R_sg.next()
                for g2 in range(0, 4, 2):
                    Pg = [R_P.next(), R_P.next()]
                    for c in range(8):
                        for k_, P in enumerate(Pg):
                            g = g2 + k_
                            op("pe", lambda e: e.matmul(P[:], lhsT=xT[:, c, :], rhs=wg[:, c, g * 512:(g + 1) * 512], start=(c == 0), stop=(c == 7)), [xT, wg], [P])
                    for k_, P in enumerate(Pg):
                        g = g2 + k_
                        op("act", lambda e: e.activation(out=sg[:, g * 512:(g + 1) * 512], in_=P[:], func=AF.Sigmoid, scale=rstd[:, 0:1]), [P, rstd], [sg])
                oaT = R_oaT.next(); dma("sp", oaT[:], oaT_d[own], reads=[b_oa[own]], writes=[oaT])
                obT = R_obT.next(); dma("sp", obT[:], obT_d[own], reads=[b_ob[own]], writes=[obT])
                t1 = R_t1.next(); t2 = R_t2.next()
                for (oT, wb_, tt, off, eng) in ((oaT, wba, t1, 0, "dve"), (obT, wbb, t2, 1024, "pool")):
                    Pg = [R_P.next(), R_P.next()]
                    for c in range(4):
                        for g, P in enumerate(Pg):
                            op("pe", lambda e: e.matmul(P[:], lhsT=oT[:, c, :], rhs=wb_[:, c, g * 512:(g + 1) * 512], start=(c == 0), stop=(c == 3)), [oT, wb_], [P])
                    for g, P in enumerate(Pg):
                        op("dve", lambda e: e.tensor_tensor(out=tt[:, g * 512:(g + 1) * 512], in0=P[:], in1=sg[:, off + g * 512:off + (g + 1) * 512], op=ALU.mult),
                           [P, sg], [tt])
                m = R_m.next()
                op("pool", lambda e: e.tensor_tensor(out=m[:], in0=t1[:], in1=t2[:], op=ALU.add), [t1, t2], [m])
                pt = PT.next()
                for c in range(8):
                    op("pe", lambda e: e.transpose(out=pt[:, c, :], in_=m[:, c * 128:(c + 1) * 128], identity=ident[:]), [m, ident], [pt])
                mT = R_mT.next()
                op("act", lambda e: e.copy(out=mT[:], in_=pt[:]), [pt], [mT])
                hh = R_h.next()
                Pg = [R_P.next(), R_P.next()]
                for c in range(8):
                    for g, P in enumerate(Pg):
                        op("pe", lambda e: e.matmul(P[:], lhsT=mT[:, c, :], rhs=wo[:, c, g * 512:(g + 1) * 512], start=(c == 0), stop=(c == 7)), [mT, wo], [P])
                for g, P in enumerate(Pg):
                    op("dve", lambda e: e.tensor_tensor(out=hh[:, g * 512:(g + 1) * 512], in0=P[:], in1=xs[:, g * 512:(g + 1) * 512], op=ALU.add), [P, xs], [hh])
                dma("pool", h_d[own], hh[:], reads=[hh], writes=[b_h[own]])
            fw.barrier()

        with ExitStack() as pes:
            wu = fw.sb("wu", [128, 8, DFF], BF16, pes)
            wd = fw.sb("wd", [128, 32, D], BF16, pes)
            stage = Rot(fw, "stgF", [128, 1024], F32, 2, pes)
            load_w(wu, w_up, 8, DFF, gffn, stage)
            load_w(wd, w_down, 32, D, None, stage)
            GT = 2
            R_h = Rot(fw, "hF", [128, D], F32, 4, pes)
            R_hb = Rot(fw, "hbF", [128, D], BF16, 2, pes)
            R_hT = Rot(fw, "hTF", [128, 8, GT * 128], BF16, 2, pes)
            junk = fw.sb("junkF", [128, D], BF16, pes)
            R_ssq = Rot(fw, "ssqF", [128, 1], F32, 4, pes)
            R_r2 = Rot(fw, "r2F", [128, 1], F32, 6, pes)
            R_r1 = Rot(fw, "r1F", [128, GT * 128], BF16, 3, pes)
            R_rT = Rot(fw, "rTF", [128, 32, GT * 128], BF16, 1, pes)
            R_y = Rot(fw, "yF", [128, D], F32, 2, pes)
            R_P = RR(PB)
            for g0 in range(0, NOWN, GT):
                tiles = list(range(g0, min(NOWN, g0 + GT)))
                nt_ = len(tiles)
                hT = R_hT.next()
                hs = []
                r2s = []
                for j, own in enumerate(tiles):
                    h = R_h.next(); dma("sp", h[:], h_d[own], reads=[b_h[own]], writes=[h])
                    hs.append(h)
                    ssq = R_ssq.next()
                    op("act", lambda e: e.activation(out=junk[:], in_=h[:], func=AF.Square, accum_out=ssq[:]), [h], [junk, ssq])
                    r2 = R_r2.next()
                    op("dve", lambda e: e.tensor_scalar(out=r2[:], in0=ssq[:], scalar1=1.0 / D, scalar2=EPS, op0=ALU.mult, op1=ALU.add), [ssq], [r2])
                    op("dve", lambda e: e.reciprocal(out=r2[:], in_=r2[:]), [r2], [r2])
                    r2s.append(r2)
                    hb = R_hb.next()
                    op("dve", lambda e: e.tensor_copy(out=hb[:], in_=h[:]), [h], [hb])
                    pt = PT.next()
                    for c in range(8):
                        op("pe", lambda e: e.transpose(out=pt[:, c, :], in_=hb[:, c * 128:(c + 1) * 128], identity=ident[:]), [hb, ident], [pt])
                    op("act", lambda e: e.copy(out=hT[:, :, j * 128:(j + 1) * 128], in_=pt[:]), [pt], [hT])
                NN = nt_ * 128
                rT = R_rT.next()
                for f0 in range(0, 32, 2):
                    Pg = [R_P.next(), R_P.next()]
                    for c in range(8):
                        for k_, P in enumerate(Pg):
                            f = f0 + k_
                            op("pe", lambda e: e.matmul(P[:, :NN], lhsT=wu[:, c, f * 128:(f + 1) * 128], rhs=hT[:, c, :NN], start=(c == 0), stop=(c == 7)), [wu, hT], [P])
                    for k_, P in enumerate(Pg):
                        f = f0 + k_
                        r1 = R_r1.next()
                        op("act", lambda e: e.activation(out=r1[:, :NN], in_=P[:, :NN], func=AF.Relu), [P], [r1])
                        op("pool", lambda e: e.tensor_tensor(out=rT[:, f, :NN], in0=r1[:, :NN], in1=r1[:, :NN], op=ALU.mult), [r1], [rT])
                for j, own in enumerate(tiles):
                    y = R_y.next()
                    Pg = [R_P.next(), R_P.next()]
                    for f in range(32):
                        for g, P in enumerate(Pg):
                            op("pe", lambda e: e.matmul(P[:], lhsT=rT[:, f, j * 128:(j + 1) * 128], rhs=wd[:, f, g * 512:(g + 1) * 512], start=(f == 0), stop=(f == 31)),
                               [rT, wd], [P])
                    for g, P in enumerate(Pg):
                        op("dve", lambda e: e.scalar_tensor_tensor(out=y[:, g * 512:(g + 1) * 512], in0=P[:], scalar=r2s[j][:, 0:1], in1=hs[j][:, g * 512:(g + 1) * 512],
                                                                   op0=ALU.mult, op1=ALU.add), [P, r2s[j], hs[j]], [y])
                    dma("pool", y_own[own], y[:], reads=[y])
            fw.barrier()
        fw.finish("sp")
        print("bass program built: %d instructions" % fw.ninst, fw.per, "nsems", len(fw.sems))
    return nc


def _rope_tab(pos):
    half = 32
    inv = np.power(np.float32(10000.0), -np.arange(half, dtype=np.float32) / np.float32(half)).astype(np.float32)
    ang = pos.astype(np.float32)[:, None] * inv[None, :]
    return np.concatenate([np.cos(ang), np.sin(ang)], axis=1).astype(np.float32)


def _consts(par, S, PAST, T):
    ident = np.eye(128, dtype=np.float32)
    ident2 = np.concatenate([ident, ident], 1)
    j = np.arange(128)[:, None]; s = np.arange(128)[None, :]
    tri = (j >= s).astype(np.float32)
    ones = np.ones((128, 128), np.float32)
    dm = np.zeros((2, 128, 256), np.float32)
    q = np.arange(128)[:, None]; k = np.arange(256)[None, :]
    qch = (par * 128 + q) // 64
    kch = k // 64
    dm[0] = np.where(kch <= qch, 0.0, MASKED)
    dm[1] = np.where(k < 128 + T, 0.0, MASKED) * np.ones((128, 1), np.float32)
    sb = np.zeros((2, 2, 128, 128), np.float32)
    kk = np.arange(128)[:, None]; qq = np.arange(128)[None, :]
    for n in range(2):
        kb_rel = 1 - n
        sb[0, n] = ((kb_rel * 128 + kk) < (par * 128 + qq)).astype(np.float32)
    sb[1, 0] = ((kk < qq) & (kk < T)).astype(np.float32)
    sb[1, 1] = 1.0
    sb2 = np.concatenate([sb, sb], axis=3)
    return dict(c_ident=ident.astype(NPBF), c_ident2=ident2.astype(NPBF), c_tri=tri.astype(NPBF), c_ones=ones.astype(NPBF),
                c_dmask=dm, c_sbm=sb2.astype(NPBF))


_PROG = {}


def run_all(inp, S, PAST, T, KP, KS):
    key = (S, PAST, KP, KS)
    if key not in _PROG:
        _PROG[key] = build_program(S, PAST, KP, KS)
    nc = _PROG[key]
    NTP = S // 128; NSP = NTP // 2; NTC = PAST // 128; NOWN = NSP + 2
    f = lambda a: np.ascontiguousarray(np.asarray(a, dtype=np.float32))
    xp = f(inp["x_prompt"]); xsm = f(inp["x_sample"])
    caches = {k: f(inp[k]) for k in ("cache_k_a", "cache_v_a", "cache_k_idx", "cache_k_sb", "cache_v_sb")}
    rope_p = _rope_tab(np.arange(S)).reshape(NTP, 128, 64)
    rs = np.zeros((128, 64), np.float32); rs[:, :32] = 1.0
    rs[:T] = _rope_tab(PAST + np.arange(T))
    in_maps = []
    for c in range(8):
        b, par = c // 2, c % 2
        m = {}
        xa = xp[b].reshape(NTP, 128, D)
        m["x_all"] = xa
        xo = np.zeros((NOWN, 128, D), np.float32)
        xo[:NSP] = xa[par::2]
        ro = np.zeros((NOWN, 128, 64), np.float32)
        ro[:NSP] = rope_p[par::2]
        for s in range(2):
            xo[NSP + s, :T] = xsm[2 * c + s]
            ro[NSP + s] = rs
        m["x_own"] = xo; m["rope_all"] = rope_p; m["rope_own"] = ro
        sl = slice(2 * c, 2 * c + 2)
        m["ck_a"] = caches["cache_k_a"][sl].reshape(2, NTC, 128, W); m["cv_a"] = caches["cache_v_a"][sl].reshape(2, NTC, 128, W)
        m["ck_i"] = caches["cache_k_idx"][sl].reshape(2, NTC, 128, 64)
        m["ck_b"] = caches["cache_k_sb"][sl].reshape(2, NTC, 128, W); m["cv_b"] = caches["cache_v_sb"][sl].reshape(2, NTC, 128, W)
        m["w_in"] = f(inp["w_in"]); m["g_mix"] = np.ascontiguousarray(f(inp["g_mix"]).reshape(8, 128).T)
        m["g_qn"] = f(inp["g_qn"]); m["g_kn"] = f(inp["g_kn"])
        m["w_ba"] = f(inp["w_branch_a"]); m["w_bb"] = f(inp["w_branch_b"]); m["w_out"] = f(inp["w_out"])
        m["g_ffn"] = np.ascontiguousarray(f(inp["g_ffn"]).reshape(8, 128).T)
        m["w_up"] = f(inp["w_up"]); m["w_down"] = f(inp["w_down"])
        m.update(_consts(par, S, PAST, T))
        in_maps.append(m)
    res = run_bass_kernel_spmd(nc, in_maps, core_ids=list(range(8)))
    R = res.results
    LAST["R"] = R
    B = 4; DB = 16
    y_p = np.zeros((B, NTP, 128, D), np.float32)
    y_s = np.zeros((DB, T, D), np.float32)
    outs_p = {k: np.zeros((B, NTP, 128, n), np.float32) for k, n in (("o_ka", W), ("o_va", W), ("o_ki", 64), ("o_kb", W), ("o_vb", W))}
    outs_s = {k: np.zeros((DB, T, n), np.float32) for k, n in (("s_ka", W), ("s_va", W), ("s_ki", 64), ("s_kb", W), ("s_vb", W))}
    for c in range(8):
        b, par = c // 2, c % 2
        r = R[c]
        y_p[b, par::2] = r["y_own"][:NSP]
        for s in range(2):
            y_s[2 * c + s] = r["y_own"][NSP + s, :T]
            for k in outs_s:
                outs_s[k][2 * c + s] = r[k][s, :T]
        for k in outs_p:
            outs_p[k][b, par::2] = r[k][par::2]
    return (y_p.reshape(B, S, D), y_s,
            outs_p["o_ka"].reshape(B, S, NH, HD), outs_p["o_va"].reshape(B, S, NH, HD), outs_p["o_ki"].reshape(B, S, 64),
            outs_p["o_kb"].reshape(B, S, NH, HD), outs_p["o_vb"].reshape(B, S, NH, HD),
            outs_s["s_ka"].reshape(DB, T, NH, HD), outs_s["s_va"].reshape(DB, T, NH, HD), outs_s["s_ki"],
            outs_s["s_kb"].reshape(DB, T, NH, HD), outs_s["s_vb"].reshape(DB, T, NH, HD))


def kernel(**inputs):
    S = inputs["x_prompt"].shape[1]
    PAST = inputs["cache_k_a"].shape[1]
    T = inputs["x_sample"].shape[1]
    return run_all(inputs, S, PAST, T, min(256, S // 4), min(256, (PAST + T) // 4))
```

```python
import numpy as np
import ml_dtypes
from contextlib import ExitStack
import concourse.bass as bass
import concourse.mybir as mybir
from concourse.bass_utils import run_bass_kernel_spmd

F32 = mybir.dt.float32
BF16 = mybir.dt.bfloat16
AF = mybir.ActivationFunctionType
ALU = mybir.AluOpType
AX = mybir.AxisListType
NPBF = ml_dtypes.bfloat16


class Buf:
    __slots__ = ("t", "lw", "rd", "name")

    def __init__(self, t, name=""):
        self.t = t
        self.lw = None
        self.rd = {}
        self.name = name

    def __getitem__(self, k):
        return self.t[k]


class FW:
    NDMA = 40

    def __init__(self, nc, es):
        self.nc = nc
        self.es = es
        self.eng = {"pe": nc.tensor, "act": nc.scalar, "dve": nc.vector, "pool": nc.gpsimd, "sp": nc.sync}
        self.sems = {}
        self.cnt = {}
        for k in self.eng:
            self.sems[k] = es.enter_context(nc.semaphore("s_" + k))
            self.cnt[k] = 0
        self.dsem = {"sp": [], "pool": []}
        for q, pre, n in (("sp", "d", 24), ("pool", "g", 20)):
            for i in range(n):
                k = "%s%d" % (pre, i)
                self.sems[k] = es.enter_context(nc.semaphore("s_" + k))
                self.cnt[k] = 0
                self.dsem[q].append(k)
        self.dnext = {"sp": 0, "pool": 0}
        self.cur = {k: k for k in self.sems}
        self.epoch = {}
        self.seen = {e: {} for e in self.eng}
        self.ninst = 0
        self.per = {}

    def sb(self, name, shape, dt, es=None):
        t = (es or self.es).enter_context(self.nc.sbuf_tensor(name, list(shape), dt))
        return Buf(t, name)

    def ps(self, name, shape, dt, es=None):
        t = (es or self.es).enter_context(self.nc.psum_tensor(name, list(shape), dt))
        return Buf(t, name)

    def _wait(self, e, tok):
        k, v = tok
        if self.seen[e].get(k, 0) >= v:
            return
        self.eng[e].wait_ge(self.sems[k], v)
        self.seen[e][k] = v
        self.ninst += 1
        self.per[e] = self.per.get(e, 0) + 1

    def _deps(self, e, reads, writes):
        for b in reads:
            if b.lw is not None and not (e == "pe" and b.lw[0].startswith("pe")):
                self._wait(e, b.lw)
        for b in writes:
            if b.lw is not None and not (e == "pe" and b.lw[0].startswith("pe")):
                self._wait(e, b.lw)
            for k, v in b.rd.items():
                if e == "pe" and k.startswith("pe"):
                    continue
                self._wait(e, (k, v))

    def _commit(self, tok, reads, writes):
        for b in writes:
            b.lw = tok
            b.rd = {}
        for b in reads:
            if b.rd.get(tok[0], 0) < tok[1]:
                b.rd[tok[0]] = tok[1]

    LIMIT = 24000

    def _roll(self, lk):
        self.epoch[lk] = self.epoch.get(lk, 0) + 1
        k = "%s#%d" % (lk, self.epoch[lk])
        self.sems[k] = self.es.enter_context(self.nc.semaphore("s_" + k.replace("#", "_")))
        self.cnt[k] = 0
        self.cur[lk] = k
        return k

    def op(self, e, fn, reads=(), writes=()):
        self._deps(e, reads, writes)
        k = self.cur[e]
        if self.cnt[k] >= self.LIMIT:
            k = self._roll(e)
        ins = fn(self.eng[e])
        self.cnt[k] += 1
        ins.then_inc(self.sems[k], 1)
        self._commit((k, self.cnt[k]), reads, writes)
        self.ninst += 1
        self.per[e] = self.per.get(e, 0) + 1
        return ins

    def dma(self, q, out, in_, reads=(), writes=()):
        self._deps(q, reads, writes)
        lk = self.dsem[q][self.dnext[q]]
        self.dnext[q] = (self.dnext[q] + 1) % len(self.dsem[q])
        k = self.cur[lk]
        if self.cnt[k] > 0:
            self._wait(q, (k, self.cnt[k]))
        if self.cnt[k] >= self.LIMIT:
            k = self._roll(lk)
        ins = self.eng[q].dma_start(out=out, in_=in_)
        self.cnt[k] += 16
        ins.then_inc(self.sems[k], 16)
        self._commit((k, self.cnt[k]), reads, writes)
        self.ninst += 1
        return ins

    def barrier(self):
        for e in self.eng:
            for k in list(self.sems):
                if k.split("#")[0] != e and self.cnt[k] > 0:
                    self._wait(e, (k, self.cnt[k]))

    def finish(self, q="sp"):
        for k in list(self.sems):
            if k.split("#")[0] != q and self.cnt[k] > 0:
                self._wait(q, (k, self.cnt[k]))


class Rot:
    def __init__(self, fw, name, shape, dt, n, es=None, psum=False, init=None):
        mk = fw.ps if psum else fw.sb
        self.bufs = [mk("%s%d" % (name, i), shape, dt, es) for i in range(n)]
        self.i = 0
        if init is not None:
            for b in self.bufs:
                init(b)

    def next(self):
        b = self.bufs[self.i]
        self.i = (self.i + 1) % len(self.bufs)
        return b


D = 1024
DFF = 4096
NH = 8
HD = 64
W = NH * HD
DIN = 5704
C_QA, C_KA, C_VA, C_QI, C_KI, C_WI, C_QB, C_KB, C_VB, C_GA, C_GB = 0, 512, 1024, 1536, 2048, 2112, 2120, 2632, 3144, 3656, 4680
NAB = 3656
EPS = 1e-6
NEG = -30000.0
MASKED = -1.0e30
WI_SCALE = float((8 ** -0.5) * (64 ** -0.5))
NBIS = 14
CNT_DVE_FRAC = 0.48
NDUMMY_D = 0
DEBUG = False
LAST = {}


def build_program(S, PAST, KP, KS):
    NTP = S // 128
    NSP = NTP // 2
    NTC = PAST // 128
    NTS = NTC + 1
    NOWN = NSP + 2
    LMAX = max(NTP, NTS) * 128

    nc = bass.Bass("TRN2", target_bir_lowering=False)

    def din(name, shape, dt=F32):
        return nc.dram_tensor(name, list(shape), dt, kind="ExternalInput").ap()

    def dout(name, shape, dt=F32):
        return nc.dram_tensor(name, list(shape), dt, kind="ExternalOutput").ap()

    def dscr(name, shape, dt=BF16):
        return nc.dram_tensor(name, list(shape), dt, kind="Internal").ap()

    x_all = din("x_all", [NTP, 128, D])
    x_own = din("x_own", [NOWN, 128, D])
    rope_all = din("rope_all", [NTP, 128, 64])
    rope_own = din("rope_own", [NOWN, 128, 64])
    ck_a = din("ck_a", [2, NTC, 128, W]); cv_a = din("cv_a", [2, NTC, 128, W])
    ck_i = din("ck_i", [2, NTC, 128, 64])
    ck_b = din("ck_b", [2, NTC, 128, W]); cv_b = din("cv_b", [2, NTC, 128, W])
    w_in = din("w_in", [D, DIN]); g_mix = din("g_mix", [128, 8])
    g_qn = din("g_qn", [64]); g_kn = din("g_kn", [64])
    w_ba = din("w_ba", [W, D]); w_bb = din("w_bb", [W, D]); w_out = din("w_out", [D, D])
    g_ffn = din("g_ffn", [128, 8]); w_up = din("w_up", [D, DFF]); w_down = din("w_down", [DFF, D])
    c_ident = din("c_ident", [128, 128], BF16); c_ident2 = din("c_ident2", [128, 256], BF16)
    c_tri = din("c_tri", [128, 128], BF16); c_ones = din("c_ones", [128, 128], BF16)
    c_dmask = din("c_dmask", [2, 128, 256])
    c_sbm = din("c_sbm", [2, 2, 128, 256], BF16)

    y_own = dout("y_own", [NOWN, 128, D])
    o_ka = dout("o_ka", [NTP, 128, W]); o_va = dout("o_va", [NTP, 128, W]); o_ki = dout("o_ki", [NTP, 128, 64])
    o_kb = dout("o_kb", [NTP, 128, W]); o_vb = dout("o_vb", [NTP, 128, W])
    s_ka = dout("s_ka", [2, 128, W]); s_va = dout("s_va", [2, 128, W]); s_ki = dout("s_ki", [2, 128, 64])
    s_kb = dout("s_kb", [2, 128, W]); s_vb = dout("s_vb", [2, 128, W])

    class Seq:
        pass

    seqs = []
    for si in range(3):
        q = Seq()
        q.idx = si
        q.nt = NTP if si == 0 else NTS
        q.kaT = dscr("kaT%d" % si, [4, 128, q.nt * 128]); q.kbT = dscr("kbT%d" % si, [4, 128, q.nt * 128])
        q.kiT = dscr("kiT%d" % si, [128, q.nt * 128])
        q.va = dscr("va%d" % si, [q.nt, 128, 520]); q.vb = dscr("vb%d" % si, [q.nt, 128, W])
        q.b_kaT = Buf(q.kaT); q.b_kbT = Buf(q.kbT); q.b_kiT = Buf(q.kiT); q.b_va = Buf(q.va); q.b_vb = Buf(q.vb)
        if si == 0:
            q.slots = [(i, 2 * i + 2) for i in range(NSP)]
            q.K = KP
            q.mi = 0
        else:
            q.slots = [(NSP + si - 1, NTS)]
            q.K = KS
            q.mi = 1
        seqs.append(q)
    qaT_d = dscr("qaT_d", [NOWN, 128, 4, 256]); qbT_d = dscr("qbT_d", [NOWN, 128, 4, 256])
    qiT_d = dscr("qiT_d", [NOWN, 128, 4, 128]); wi_d = dscr("wi_d", [NOWN, 128, 8], F32)
    negm_d = (dout if DEBUG else dscr)("negm_d", [NOWN, 128, LMAX], BF16)
    dbg = dout if DEBUG else dscr
    oaT_d = dbg("oaT_d", [NOWN, 128, 4, 128], BF16); obT_d = dbg("obT_d", [NOWN, 128, 4, 128], BF16)
    h_d = dbg("h_d", [NOWN, 128, D], F32)
    b_q = [Buf(None) for _ in range(NOWN)]
    b_negm = [Buf(None) for _ in range(NOWN)]
    b_oa = [Buf(None) for _ in range(NOWN)]
    b_ob = [Buf(None) for _ in range(NOWN)]
    b_h = [Buf(None) for _ in range(NOWN)]

    es = ExitStack()
    with es:
        fw = FW(nc, es)
        op, dma = fw.op, fw.dma
        ident = fw.sb("ident", [128, 128], BF16); ident2 = fw.sb("ident2", [128, 256], BF16)
        tri = fw.sb("tri", [128, 128], BF16); ones = fw.sb("ones", [128, 128], BF16)
        dmask = fw.sb("dmask", [128, 2, 256], F32); sbm = fw.sb("sbm", [128, 4, 256], BF16)
        gq = fw.sb("gq", [128, 64], F32); gk = fw.sb("gk", [128, 64], F32)
        gmix = fw.sb("gmix", [128, 8], F32); gffn = fw.sb("gffn", [128, 8], F32)
        dma("sp", ident[:], c_ident[:, :], writes=[ident]); dma("sp", ident2[:], c_ident2[:, :], writes=[ident2])
        dma("sp", tri[:], c_tri[:, :], writes=[tri]); dma("sp", ones[:], c_ones[:, :], writes=[ones])
        dma("sp", dmask[:], c_dmask.rearrange("a p n -> p a n"), writes=[dmask])
        dma("sp", sbm[:], c_sbm.rearrange("a b p n -> p (a b) n"), writes=[sbm])
        dma("sp", gq[:], g_qn.partition_broadcast(128), writes=[gq]); dma("sp", gk[:], g_kn.partition_broadcast(128), writes=[gk])
        dma("sp", gmix[:], g_mix[:, :], writes=[gmix]); dma("sp", gffn[:], g_ffn[:, :], writes=[gffn])
        PT = Rot(fw, "PT", [128, 8, 128], BF16, 2, psum=True)
        PB = [fw.ps("PB%d" % i, [128, 512], F32) for i in range(6)]

        class RR:
            def __init__(self, bufs):
                self.bufs = bufs; self.i = 0

            def next(self):
                b = self.bufs[self.i]; self.i = (self.i + 1) % len(self.bufs); return b

        def load_w(dst, src, C, n, gvec, stage, coloff=0):
            for c in range(C):
                for lo in range(0, n, 1024):
                    m = min(1024, n - lo)
                    st = stage.next()
                    dma("sp", st[:, :m], src[c * 128:(c + 1) * 128, lo:lo + m], writes=[st])
                    if gvec is not None:
                        op("act", lambda e: e.activation(out=dst[:, c, coloff + lo:coloff + lo + m], in_=st[:, :m], func=AF.Copy,
                                                         scale=gvec[:, c:c + 1]), [st, gvec], [dst])
                    else:
                        op("dve", lambda e: e.tensor_copy(out=dst[:, c, coloff + lo:coloff + lo + m], in_=st[:, :m]), [st], [dst])

        with ExitStack() as pes:
            wab = fw.sb("wab", [128, 8, NAB], BF16, pes)
            stage = Rot(fw, "stg", [128, 1024], F32, 2, pes)
            load_w(wab, w_in[:, 0:NAB], 8, NAB, gmix, stage)
            R_xs = Rot(fw, "xs", [128, D], F32, 4, pes)
            R_xb = Rot(fw, "xb", [128, D], BF16, 4, pes)
            R_xT = Rot(fw, "xT", [128, 8, 128], BF16, 4, pes)
            junk = fw.sb("junk", [128, D], BF16, pes)
            R_ssq = Rot(fw, "ssq", [128, 1], F32, 2, pes)
            R_rstd = Rot(fw, "rstd", [128, 1], F32, 5, pes)
            R_rope = Rot(fw, "rope", [128, 64], F32, 5, pes)
            R_z = Rot(fw, "z", [128, W], F32, 10, pes)
            R_zo = Rot(fw, "zo", [128, W], F32, 8, pes)
            R_zb = Rot(fw, "zb", [128, W], BF16, 10, pes)
            R_t = Rot(fw, "tt", [128, 256], F32, 12, pes)
            R_sq = Rot(fw, "sqh", [128, W], F32, 2, pes)
            R_s8 = Rot(fw, "s8", [128, 8], F32, 4, pes)
            R_kT = Rot(fw, "kT", [128, 4, 128], BF16, 5, pes)
            R_kiT = Rot(fw, "kiTt", [64, 128], BF16, 2, pes)
            R_wi = Rot(fw, "wit", [128, 8], F32, 2, pes)
            R_P = RR(PB)

            def zero_init(b):
                op("pool", lambda e: e.memset(b[:], 0.0), [], [b])

            def ones_init(b):
                op("pool", lambda e: e.memset(b[:], 1.0), [], [b])

            R_qbd = Rot(fw, "qbd", [128, 4, 256], BF16, 3, pes, init=zero_init)
            R_vat = Rot(fw, "vat", [128, 8, 65], BF16, 5, pes, init=ones_init)
            R_vbt = Rot(fw, "vbt", [128, W], BF16, 5, pes)

            def front(x_ap, rope_ap):
                xs = R_xs.next()
                dma("sp", xs[:], x_ap, writes=[xs])
                rp = R_rope.next()
                dma("sp", rp[:], rope_ap, writes=[rp])
                ssq = R_ssq.next()
                op("act", lambda e: e.activation(out=junk[:], in_=xs[:], func=AF.Square, accum_out=ssq[:]), [xs], [junk, ssq])
                rstd = R_rstd.next()
                op("act", lambda e: e.activation(out=rstd[:], in_=ssq[:], func=AF.Sqrt, scale=1.0 / D, bias=EPS), [ssq], [rstd])
                op("dve", lambda e: e.reciprocal(out=rstd[:], in_=rstd[:]), [rstd], [rstd])
                xb = R_xb.next()
                op("act", lambda e: e.copy(out=xb[:], in_=xs[:]), [xs], [xb])
                pt = PT.bufs[0]
                for c in range(8):
                    op("pe", lambda e: e.transpose(out=pt[:, c, :], in_=xb[:, c * 128:(c + 1) * 128], identity=ident[:]), [xb, ident], [pt])
                xT = R_xT.next()
                op("act", lambda e: e.copy(out=xT[:], in_=pt[:]), [pt], [xT])
                return xs, xT, rstd, rp

            def proj(xT, lo, n):
                P = R_P.next()
                for c in range(8):
                    op("pe", lambda e: e.matmul(P[:, :n], lhsT=xT[:, c, :], rhs=wab[:, c, lo:lo + n], start=(c == 0), stop=(c == 7)),
                       [xT, wab], [P])
                return P

            def evac(P, n, rstd):
                z = R_z.next()
                op("act", lambda e: e.activation(out=z[:, :n], in_=P[:, :n], func=AF.Copy, scale=rstd[:, 0:1]), [P, rstd], [z])
                return z

            def v3(ap, h):
                return ap.rearrange("p (h d) -> p h d", h=h)

            def headnorm(z, g):
                sq = R_sq.next()
                op("act", lambda e: e.activation(out=sq[:], in_=z[:], func=AF.Square), [z], [sq])
                s8 = R_s8.next()
                op("dve", lambda e: e.tensor_reduce(out=s8[:], in_=v3(sq[:], 8), axis=AX.X, op=ALU.add), [sq], [s8])
                op("act", lambda e: e.activation(out=s8[:], in_=s8[:], func=AF.Sqrt, scale=1.0 / HD, bias=EPS), [s8], [s8])
                op("dve", lambda e: e.reciprocal(out=s8[:], in_=s8[:]), [s8], [s8])
                zn = R_zo.next()
                op("dve", lambda e: e.tensor_tensor(out=v3(zn[:], 8), in0=v3(z[:], 8), in1=s8[:].unsqueeze(2).to_broadcast([128, 8, HD]),
                                                    op=ALU.mult), [z, s8], [zn])
                op("dve", lambda e: e.tensor_tensor(out=v3(zn[:], 8), in0=v3(zn[:], 8), in1=g[:].unsqueeze(1).to_broadcast([128, 8, HD]),
                                                    op=ALU.mult), [zn, g], [zn])
                return zn

            def rope(z, rp, H):
                n = H * HD
                o = R_zo.next()
                zv = z[:, :n].rearrange("p (h t d) -> p h t d", h=H, t=2)
                ov = o[:, :n].rearrange("p (h t d) -> p h t d", h=H, t=2)
                cos = rp[:, 0:32].unsqueeze(1).to_broadcast([128, H, 32])
                sin = rp[:, 32:64].unsqueeze(1).to_broadcast([128, H, 32])
                t1, t2, t3, t4 = R_t.next(), R_t.next(), R_t.next(), R_t.next()

                def tv(t):
                    return t[:, :H * 32].rearrange("p (h d) -> p h d", h=H)
                e1 = "dve"
                e2 = "pool" if H == 1 else "dve"
                op(e1, lambda e: e.tensor_tensor(out=tv(t1), in0=zv[:, :, 0, :], in1=cos, op=ALU.mult), [z, rp], [t1])
                op(e2, lambda e: e.tensor_tensor(out=tv(t2), in0=zv[:, :, 1, :], in1=sin, op=ALU.mult), [z, rp], [t2])
                op(e1, lambda e: e.tensor_tensor(out=ov[:, :, 0, :], in0=tv(t1), in1=tv(t2), op=ALU.subtract), [t1, t2], [o])
                op(e2, lambda e: e.tensor_tensor(out=tv(t3), in0=zv[:, :, 1, :], in1=cos, op=ALU.mult), [z, rp], [t3])
                op(e1, lambda e: e.tensor_tensor(out=tv(t4), in0=zv[:, :, 0, :], in1=sin, op=ALU.mult), [z, rp], [t4])
                op(e2, lambda e: e.tensor_tensor(out=ov[:, :, 1, :], in0=tv(t3), in1=tv(t4), op=ALU.add), [t3, t4], [o])
                return o

            def to_T(z, n):
                zb = R_zb.next()
                op("act", lambda e: e.copy(out=zb[:, :n], in_=z[:, :n]), [z], [zb])
                pt = PT.next()
                if n == 64:
                    op("pe", lambda e: e.transpose(out=pt[0:64, 0, :], in_=zb[:, 0:64], identity=ident[:]), [zb, ident], [pt])
                else:
                    for j in range(n // 128):
                        op("pe", lambda e: e.transpose(out=pt[:, j, :], in_=zb[:, j * 128:(j + 1) * 128], identity=ident[:]), [zb, ident], [pt])
                return pt

            def emit_kT(z, dstT, bdst, t):
                pt = to_T(z, W)
                kT = R_kT.next()
                op("dve", lambda e: e.tensor_copy(out=kT[:], in_=pt[:, 0:4, :]), [pt], [kT])
                dma("pool", dstT.rearrange("a p n -> p a n")[:, :, t * 128:(t + 1) * 128], kT[:], reads=[kT], writes=[bdst])

            def emit_kiT(z, seq, t):
                pt = to_T(z, 64)
                kt = R_kiT.next()
                op("act", lambda e: e.copy(out=kt[:], in_=pt[0:64, 0, :]), [pt], [kt])
                dma("pool", seq.kiT[0:64, t * 128:(t + 1) * 128], kt[:], reads=[kt], writes=[seq.b_kiT])
                dma("pool", seq.kiT[64:128, t * 128:(t + 1) * 128], kt[:], reads=[kt], writes=[seq.b_kiT])

            def emit_va(z, seq, t):
                vt = R_vat.next()
                op("dve", lambda e: e.tensor_copy(out=vt[:, :, 0:64], in_=v3(z[:], 8)), [z], [vt])
                dma("pool", seq.va[t], vt[:].rearrange("p h d -> p (h d)"), reads=[vt], writes=[seq.b_va])

            def emit_vb(z, seq, t):
                vt = R_vbt.next()
                op("dve", lambda e: e.tensor_copy(out=vt[:], in_=z[:]), [z], [vt])
                dma("pool", seq.vb[t], vt[:], reads=[vt], writes=[seq.b_vb])

            def emit_qbd(z, dst, own):
                pt = to_T(z, W)
                qb = R_qbd.next()
                op("act", lambda e: e.copy(out=qb[0:64, :, 0:128], in_=pt[0:64, 0:4, :]), [pt], [qb])
                op("dve", lambda e: e.tensor_copy(out=qb[64:128, :, 128:256], in_=pt[64:128, 0:4, :]), [pt], [qb])
                dma("pool", dst[own], qb[:], reads=[qb], writes=[b_q[own]])

            R_kT8 = Rot(fw, "kT8", [128, 8, 128], BF16, 3, pes)

            def cast_bf(z, n):
                zb = R_zb.next()
                op("dve", lambda e: e.tensor_copy(out=zb[:, :n], in_=z[:, :n]), [z], [zb])
                return zb

            def proj_multi(xT, specs):
                Ps = [R_P.next() for _ in specs]
                for c in range(8):
                    for P, (lo, n) in zip(Ps, specs):
                        op("pe", lambda e: e.matmul(P[:, :n], lhsT=xT[:, c, :], rhs=wab[:, c, lo:lo + n], start=(c == 0), stop=(c == 7)),
                           [xT, wab], [P])
                return Ps

            def kside_proj(xT):
                return proj_multi(xT, [(C_KA, W), (C_VA, W), (C_KI, 64)]) + proj_multi(xT, [(C_KB, W), (C_VB, W)])

            def kside_post(seq, t, Ps, rstd, rp, outs):
                oka, ova, oki, okb, ovb = outs
                z_ka = evac(Ps[0], W, rstd); z_va = evac(Ps[1], W, rstd); z_ki = evac(Ps[2], 64, rstd)
                z_kb = evac(Ps[3], W, rstd); z_vb = evac(Ps[4], W, rstd)
                o_ka = rope(headnorm(z_ka, gk), rp, 8)
                dma("pool", oka, o_ka[:], reads=[o_ka])
                zb_ka = cast_bf(o_ka, W)
                dma("pool", okb, z_kb[:], reads=[z_kb])
                zb_kb = cast_bf(z_kb, W)
                o_ki = rope(z_ki, rp, 1)
                dma("pool", oki, o_ki[:, 0:64], reads=[o_ki])
                zb_ki = cast_bf(o_ki, 64)
                dma("pool", ova, z_va[:], reads=[z_va])
                emit_va(z_va, seq, t)
                dma("pool", ovb, z_vb[:], reads=[z_vb])
                emit_vb(z_vb, seq, t)
                return (seq, t, zb_ka, zb_kb, zb_ki)

            def kside_T(seq, t, zb_ka, zb_kb, zb_ki):
                pt = PT.bufs[1]
                for j in range(4):
                    op("pe", lambda e: e.transpose(out=pt[:, j, :], in_=zb_ka[:, j * 128:(j + 1) * 128], identity=ident[:]), [zb_ka, ident], [pt])
                for j in range(4):
                    op("pe", lambda e: e.transpose(out=pt[:, 4 + j, :], in_=zb_kb[:, j * 128:(j + 1) * 128], identity=ident[:]), [zb_kb, ident], [pt])
                k8 = R_kT8.next()
                op("act", lambda e: e.copy(out=k8[:], in_=pt[:]), [pt], [k8])
                dma("pool", seq.kaT.rearrange("a p n -> p a n")[:, :, t * 128:(t + 1) * 128], k8[:, 0:4, :], reads=[k8], writes=[seq.b_kaT])
                dma("pool", seq.kbT.rearrange("a p n -> p a n")[:, :, t * 128:(t + 1) * 128], k8[:, 4:8, :], reads=[k8], writes=[seq.b_kbT])
                pt2 = PT.bufs[1]
                op("pe", lambda e: e.transpose(out=pt2[0:64, 0, :], in_=zb_ki[:, 0:64], identity=ident[:]), [zb_ki, ident], [pt2])
                kt = R_kiT.next()
                op("act", lambda e: e.copy(out=kt[:], in_=pt2[0:64, 0, :]), [pt2], [kt])
                dma("pool", seq.kiT[0:64, t * 128:(t + 1) * 128], kt[:], reads=[kt], writes=[seq.b_kiT])
                dma("pool", seq.kiT[64:128, t * 128:(t + 1) * 128], kt[:], reads=[kt], writes=[seq.b_kiT])

            def qside_proj(xT):
                return proj_multi(xT, [(C_QA, W), (C_QI, W)]) + proj_multi(xT, [(C_WI, 8), (C_QB, W)])

            def qside_post(own, Ps, rstd, rp):
                z_qa = evac(Ps[0], W, rstd); z_qi = evac(Ps[1], W, rstd)
                wt = R_wi.next()
                op("dve", lambda e: e.tensor_scalar(out=wt[:], in0=Ps[2][:, 0:8], scalar1=rstd[:, 0:1], scalar2=WI_SCALE, op0=ALU.mult, op1=ALU.mult),
                   [Ps[2], rstd], [wt])
                z_qb = evac(Ps[3], W, rstd)
                dma("pool", wi_d[own], wt[:], reads=[wt], writes=[b_q[own]])
                zb_qb = cast_bf(z_qb, W)
                o_qa = rope(headnorm(z_qa, gq), rp, 8)
                zb_qa = cast_bf(o_qa, W)
                o_qi = rope(z_qi, rp, 8)
                zb_qi = cast_bf(o_qi, W)
                return (own, zb_qa, zb_qb, zb_qi)

            def qside_T(own, zb_qa, zb_qb, zb_qi):
                pt = PT.bufs[1]
                for j in range(4):
                    op("pe", lambda e: e.transpose(out=pt[:, j, :], in_=zb_qa[:, j * 128:(j + 1) * 128], identity=ident[:]), [zb_qa, ident], [pt])
                for j in range(4):
                    op("pe", lambda e: e.transpose(out=pt[:, 4 + j, :], in_=zb_qb[:, j * 128:(j + 1) * 128], identity=ident[:]), [zb_qb, ident], [pt])
                for (o4, dst) in ((0, qaT_d), (4, qbT_d)):
                    qb = R_qbd.next()
                    op("act", lambda e: e.copy(out=qb[0:64, :, 0:128], in_=pt[0:64, o4:o4 + 4, :]), [pt], [qb])
                    op("act", lambda e: e.copy(out=qb[64:128, :, 128:256], in_=pt[64:128, o4:o4 + 4, :]), [pt], [qb])
                    dma("pool", dst[own], qb[:], reads=[qb], writes=[b_q[own]])
                pt2 = PT.bufs[1]
                for j in range(4):
                    op("pe", lambda e: e.transpose(out=pt2[:, j, :], in_=zb_qi[:, j * 128:(j + 1) * 128], identity=ident[:]), [zb_qi, ident], [pt2])
                kT = R_kT.next()
                op("act", lambda e: e.copy(out=kT[:], in_=pt2[:, 0:4, :]), [pt2], [kT])
                dma("pool", qiT_d[own], kT[:], reads=[kT], writes=[b_q[own]])

            sq0 = seqs[0]
            pend = None
            frq = [front(x_all[u], rope_all[u]) for u in range(min(2, NTP))]
            for t in range(NTP):
                xs, xT, rstd, rp = frq.pop(0)
                Ps = kside_proj(xT)
                if t + 2 < NTP:
                    frq.append(front(x_all[t + 2], rope_all[t + 2]))
                if pend is not None:
                    kside_T(*pend)
                pend = kside_post(sq0, t, Ps, rstd, rp, (o_ka[t], o_va[t], o_ki[t], o_kb[t], o_vb[t]))
            kside_T(*pend)
            pend = None
            frq = [front(x_own[u], rope_own[u]) for u in range(min(2, NSP))]
            for i in range(NSP):
                xs, xT, rstd, rp = frq.pop(0)
                Ps = qside_proj(xT)
                if i + 2 < NSP:
                    frq.append(front(x_own[i + 2], rope_own[i + 2]))
                if pend is not None:
                    qside_T(*pend)
                pend = qside_post(i, Ps, rstd, rp)
            qside_T(*pend)
            for s in range(2):
                sq = seqs[1 + s]
                pend = None
                for t in range(NTC):
                    z = R_z.next(); dma("sp", z[:], ck_a[s, t], writes=[z]); zb_ka = cast_bf(z, W)
                    z = R_z.next(); dma("sp", z[:], ck_b[s, t], writes=[z]); zb_kb = cast_bf(z, W)
                    z = R_z.next(); dma("sp", z[:, 0:64], ck_i[s, t], writes=[z]); zb_ki = cast_bf(z, 64)
                    z = R_z.next(); dma("sp", z[:], cv_a[s, t], writes=[z]); emit_va(z, sq, t)
                    z = R_z.next(); dma("sp", z[:], cv_b[s, t], writes=[z]); emit_vb(z, sq, t)
                    if pend is not None:
                        kside_T(*pend)
                    pend = (sq, t, zb_ka, zb_kb, zb_ki)
                kside_T(*pend)
                own = NSP + s
                xs, xT, rstd, rp = front(x_own[own], rope_own[own])
                Ps = kside_proj(xT)
                kside_T(*kside_post(sq, NTC, Ps, rstd, rp, (s_ka[s], s_va[s], s_ki[s], s_kb[s], s_vb[s])))
                Ps = qside_proj(xT)
                qside_T(*qside_post(own, Ps, rstd, rp))
            fw.barrier()

        NTM = max(NTP, NTS)
        NVC = 4
        VCH = (NTM + NVC - 1) // NVC

        def gen_c2(seq, own, nkb, prs, nm, kaT, vaR, acc, LG, R_q, R_p, R_oa, R_rd, R_oT, hbase):
            qa = R_q.next(); dma("sp", qa[:], qaT_d[own], reads=[b_q[own]], writes=[qa])
            op("dve", lambda e: e.memset(acc[:], 0.0), [], [acc])
            pend = []

            def stage1(pr, kb):
                lg = LG.next()
                op("pe", lambda e: e.matmul(lg[:, 0:256], lhsT=kaT[pr][:, kb * 128:(kb + 1) * 128], rhs=qa[:, pr, :], start=True, stop=False),
                   [kaT[pr], qa], [lg])
                op("pe", lambda e: e.matmul(lg[:, 0:256], lhsT=nm[:, kb * 128:(kb + 1) * 128], rhs=ident2[:], start=False, stop=True),
                   [nm, ident2], [lg])
                p = R_p.next()
                op("act", lambda e: e.activation(out=p[:], in_=lg[:, 0:256], func=AF.Exp, scale=0.125), [lg], [p])
                return p

            def stage2(pr, kb, p):
                vb_ = vaR[kb // VCH]
                kk = kb % VCH
                for hh in range(2):
                    hl = 2 * pr + hh - hbase
                    c = hl * 65
                    op("pe", lambda e: e.matmul(acc[:, c:c + 65], lhsT=p[:, hh * 128:(hh + 1) * 128], rhs=vb_[:, kk, hl * 65:(hl + 1) * 65],
                                                start=False, stop=False, skip_group_check=True), [p, vb_], [acc])

            for pr in prs:
                for kb in range(nkb):
                    p = stage1(pr, kb)
                    pend.append((pr, kb, p))
                    if len(pend) > 2:
                        stage2(*pend.pop(0))
                    yield
            while pend:
                stage2(*pend.pop(0))
            oa = R_oa.next()
            rd = R_rd.next()
            av = acc[:, 0:260].rearrange("p (h d) -> p h d", h=4)
            op("dve", lambda e: e.reciprocal(out=rd[:], in_=av[:, :, 64]), [acc], [rd])
            op("dve", lambda e: e.tensor_tensor(out=oa[:].rearrange("p (h d) -> p h d", h=4), in0=av[:, :, 0:64],
                                                in1=rd[:].unsqueeze(2).to_broadcast([128, 4, 64]), op=ALU.mult), [acc, rd], [oa])
            pt = PT.next()
            for j in range(2):
                op("pe", lambda e: e.transpose(out=pt[:, j, :], in_=oa[:, j * 128:(j + 1) * 128], identity=ident[:]), [oa, ident], [pt])
            oT = R_oT.next()
            op("act", lambda e: e.copy(out=oT[:], in_=pt[:, 0:2, :]), [pt], [oT])
            c0 = hbase // 2
            dma("pool", oaT_d[own][:, c0:c0 + 2, :], oT[:], reads=[oT], writes=[b_oa[own]])
            yield

        def interleave(g1, g2, n1, n2):
            d1 = d2 = 0
            a1 = a2 = True
            while a1 or a2:
                if a1 and (not a2 or d1 * n2 <= d2 * n1):
                    try:
                        next(g1); d1 += 1
                    except StopIteration:
                        a1 = False
                else:
                    try:
                        next(g2); d2 += 1
                    except StopIteration:
                        a2 = False

        with ExitStack() as pes:
            NCH = (LMAX + 2047) // 2048
            kiT = [fw.sb("kiTr%d" % j, [128, 2048], BF16, pes) for j in range(NCH)]
            sc = fw.sb("score", [128, LMAX], F32, pes)
            R_nm = Rot(fw, "negm", [128, LMAX], BF16, 2, pes)
            sjunk = fw.sb("sjunk", [128, LMAX // 2], BF16, pes)
            sjunk2 = fw.sb("sjunk2", [128, LMAX * 3 // 4 + 256], BF16, pes)
            R_r = Rot(fw, "relu", [128, 512], F32, 3, pes)
            R_qi = Rot(fw, "qiTs", [128, 4, 128], BF16, 2, pes)
            R_wi2 = Rot(fw, "wis", [128, 8], F32, 2, pes)
            R_sm = Rot(fw, "sm", [128, 1], F32, 24, pes)
            R_w0 = Rot(fw, "smw", [128, 1], F32, 4, pes)
            R_Pc = RR(PB[4:6])
            kaTA = {pr: fw.sb("kaTA%d" % pr, [128, LMAX], BF16, pes) for pr in (0, 1)}
            vaA = [fw.sb("vaA%d" % j, [128, VCH, 260], BF16, pes) for j in range(NVC)]
            R_qA = Rot(fw, "qaTsA", [128, 4, 256], BF16, 2, pes)
            R_pA = Rot(fw, "pTA", [128, 256], BF16, 6, pes)
            R_oaA = Rot(fw, "oaA", [128, 256], BF16, 2, pes)
            R_rdA = Rot(fw, "rdenA", [128, 4], F32, 2, pes)
            R_oTA = Rot(fw, "oaTA", [128, 2, 128], BF16, 2, pes)
            LGA = RR(PB[0:2]); ACCA = RR(PB[2:4])

            def gen_c1(seq, own, nkb, holder):
                L = nkb * 128
                qi = R_qi.next(); dma("sp", qi[:], qiT_d[own], reads=[b_q[own]], writes=[qi])
                wi = R_wi2.next(); dma("sp", wi[:], wi_d[own], reads=[b_q[own]], writes=[wi])
                for g0 in range(0, L, 512):
                    n = min(512, L - g0)
                    kb_ = kiT[g0 // 2048]
                    c0 = g0 % 2048
                    for h in range(8):
                        pb = (h % 2) * 64
                        P = R_Pc.next()
                        op("pe", lambda e: e.matmul(P[:, :n], lhsT=qi[pb:pb + 64, h // 2, :], rhs=kb_[pb:pb + 64, c0:c0 + n], start=True, stop=True),
                           [qi, kb_], [P])
                        r = R_r.next()
                        op("act", lambda e: e.activation(out=r[:, :n], in_=P[:, :n], func=AF.Relu), [P], [r])
                        if h == 0:
                            op("dve", lambda e: e.tensor_scalar(out=sc[:, g0:g0 + n], in0=r[:, :n], scalar1=wi[:, 0:1], scalar2=None, op0=ALU.mult),
                               [r, wi], [sc])
                        else:
                            op("dve", lambda e: e.scalar_tensor_tensor(out=sc[:, g0:g0 + n], in0=r[:, :n], scalar=wi[:, h:h + 1], in1=sc[:, g0:g0 + n],
                                                                       op0=ALU.mult, op1=ALU.add), [r, wi, sc], [sc])
                    yield
                op("dve", lambda e: e.tensor_tensor(out=sc[:, L - 256:L], in0=sc[:, L - 256:L], in1=dmask[:, seq.mi, :], op=ALU.add), [sc, dmask], [sc])
                lo = R_sm.next()
                if L - 256 < seq.K:
                    op("dve", lambda e: e.memset(lo[:], -1.0e29), [], [lo])
                else:
                    hi = R_w0.next(); w0 = R_w0.next()
                    op("dve", lambda e: e.tensor_reduce(out=hi[:], in_=sc[:, :L], axis=AX.X, op=ALU.max), [sc], [hi])
                    op("dve", lambda e: e.tensor_reduce(out=lo[:], in_=sc[:, :L - 256], axis=AX.X, op=ALU.min), [sc], [lo])
                    op("dve", lambda e: e.tensor_tensor(out=w0[:], in0=hi[:], in1=lo[:], op=ALU.subtract), [hi, lo], [w0])
                    yield
                    Lh = max(128, int(L * CNT_DVE_FRAC) // 128 * 128)
                    nact = L - Lh
                    for it in range(NBIS):
                        f = 2.0 ** -(it + 1)
                        mid = R_sm.next(); cnt = R_sm.next(); sg_ = R_sm.next(); tt_ = R_sm.next(); pred = R_sm.next(); lo2 = R_sm.next()
                        op("dve", lambda e: e.scalar_tensor_tensor(out=mid[:], in0=w0[:], scalar=f, in1=lo[:], op0=ALU.mult, op1=ALU.add),
                           [w0, lo], [mid])
                        op("act", lambda e: e.activation(out=sjunk2[:, :nact], in_=sc[:, Lh:L], func=AF.Sign, scale=-1.0, bias=mid[:, 0:1], accum_out=sg_[:]),
                           [sc, mid], [sjunk2, sg_])
                        op("dve", lambda e: e.tensor_scalar(out=sjunk[:, :Lh], in0=sc[:, :Lh], scalar1=mid[:, 0:1], scalar2=0.0, op0=ALU.is_ge, op1=ALU.add,
                                                            accum_out=cnt[:]), [sc, mid], [sjunk, cnt])
                        op("dve", lambda e: e.scalar_tensor_tensor(out=tt_[:], in0=sg_[:], scalar=-0.5, in1=cnt[:], op0=ALU.mult, op1=ALU.add),
                           [sg_, cnt], [tt_])
                        op("dve", lambda e: e.tensor_scalar(out=pred[:], in0=tt_[:], scalar1=float(seq.K) - nact / 2.0, scalar2=f, op0=ALU.is_ge, op1=ALU.mult),
                           [tt_], [pred])
                        op("dve", lambda e: e.scalar_tensor_tensor(out=lo2[:], in0=pred[:], scalar=w0[:, 0:1], in1=lo[:], op0=ALU.mult, op1=ALU.add),
                           [pred, w0, lo], [lo2])
                        lo = lo2
                        yield
                nm = R_nm.next()
                op("dve", lambda e: e.tensor_scalar(out=nm[:, :L], in0=sc[:, :L], scalar1=lo[:, 0:1], scalar2=NEG, op0=ALU.is_lt, op1=ALU.mult),
                   [sc, lo], [nm])
                dma("pool", negm_d[own, :, 0:L], nm[:, :L], reads=[nm], writes=[b_negm[own]])
                holder.append(nm)
                yield

            for seq in seqs:
                L_all = seq.nt * 128
                for j in range((L_all + 2047) // 2048):
                    m = min(2048, L_all - j * 2048)
                    dma("sp", kiT[j][:, :m], seq.kiT[:, j * 2048:j * 2048 + m], reads=[seq.b_kiT], writes=[kiT[j]])
                for pr in (0, 1):
                    dma("sp", kaTA[pr][:, :L_all], seq.kaT[pr], reads=[seq.b_kaT], writes=[kaTA[pr]])
                for j in range(NVC):
                    t0 = j * VCH
                    m = min(VCH, seq.nt - t0)
                    for u in range(0, max(m, 0), 8):
                        mm = min(8, m - u)
                        dma("sp", vaA[j][:, u:u + mm, :], seq.va[t0 + u:t0 + u + mm, :, 0:260].rearrange("t p n -> p t n"), reads=[seq.b_va], writes=[vaA[j]])
                masks = {}
                prev = None
                for (own, nkb) in seq.slots:
                    L = nkb * 128
                    hold = []
                    g1 = gen_c1(seq, own, nkb, hold)
                    n1 = (L + 511) // 512 + NBIS + 2
                    if prev is None:
                        for _ in g1:
                            pass
                    else:
                        pown, pnkb, pnm = prev
                        g2 = gen_c2(seq, pown, pnkb, (0, 1), pnm, kaTA, vaA, ACCA.next(), LGA, R_qA, R_pA, R_oaA, R_rdA, R_oTA, 0)
                        interleave(g1, g2, n1, 2 * pnkb + 1)
                    prev = (own, nkb, hold[0])
                pown, pnkb, pnm = prev
                for _ in gen_c2(seq, pown, pnkb, (0, 1), pnm, kaTA, vaA, ACCA.next(), LGA, R_qA, R_pA, R_oaA, R_rdA, R_oTA, 0):
                    pass
            fw.barrier()

        with ExitStack() as pes:
            kaTB = {pr: fw.sb("kaTB%d" % pr, [128, LMAX], BF16, pes) for pr in (2, 3)}
            vaB = [fw.sb("vaB%d" % j, [128, VCH, 260], BF16, pes) for j in range(NVC)]
            R_nm2 = Rot(fw, "negm2", [128, LMAX], BF16, 2, pes)
            R_qB = Rot(fw, "qaTsB", [128, 4, 256], BF16, 2, pes)
            R_pB = Rot(fw, "pTB", [128, 256], BF16, 6, pes)
            R_oaB = Rot(fw, "oaB", [128, 256], BF16, 2, pes)
            R_rdB = Rot(fw, "rdenB", [128, 4], F32, 2, pes)
            R_oTB = Rot(fw, "oaTB", [128, 2, 128], BF16, 2, pes)
            LGB = RR(PB[0:3]); ACCB = RR(PB[3:6])
            for seq in seqs:
                L_all = seq.nt * 128
                for pr in (2, 3):
                    dma("sp", kaTB[pr][:, :L_all], seq.kaT[pr], reads=[seq.b_kaT], writes=[kaTB[pr]])
                for j in range(NVC):
                    t0 = j * VCH
                    m = min(VCH, seq.nt - t0)
                    for u in range(0, max(m, 0), 8):
                        mm = min(8, m - u)
                        dma("sp", vaB[j][:, u:u + mm, :], seq.va[t0 + u:t0 + u + mm, :, 260:520].rearrange("t p n -> p t n"), reads=[seq.b_va], writes=[vaB[j]])
                for (own, nkb) in seq.slots:
                    L = nkb * 128
                    nm = R_nm2.next(); dma("sp", nm[:, :L], negm_d[own, :, 0:L], reads=[b_negm[own]], writes=[nm])
                    for _ in gen_c2(seq, own, nkb, (2, 3), nm, kaTB, vaB, ACCB.next(), LGB, R_qB, R_pB, R_oaB, R_rdB, R_oTB, 4):
                        pass
            fw.barrier()

        with ExitStack() as pes:
            kbT = [fw.sb("kbTr%d" % j, [128, LMAX], BF16, pes) for j in range(4)]
            vbR = [fw.sb("vbR%d" % j, [128, VCH, W], BF16, pes) for j in range(NVC)]
            R_q = Rot(fw, "qbTs", [128, 4, 256], BF16, 2, pes)
            R_e = Rot(fw, "sbe", [128, 512], F32, 4, pes)
            R_sp = Rot(fw, "sbsp", [128, 512], BF16, 4, pes)
            R_f = Rot(fw, "sbf", [128, 512], F32, 3, pes)
            R_a = Rot(fw, "sba", [128, 512], BF16, 4, pes)
            R_S = Rot(fw, "sbS", [128, 512], BF16, 4, pes)
            R_ob = Rot(fw, "ob", [128, W], BF16, 2, pes)
            R_oT = Rot(fw, "obT", [128, 4, 128], BF16, 2, pes)
            ZB = RR(PB[0:2]); XB = RR(PB[2:4]); ACC = RR(PB[4:6])
            ptd = PT.bufs[1]
            for seq in seqs:
                L_all = seq.nt * 128
                for j in range(4):
                    dma("sp", kbT[j][:, :L_all], seq.kbT[j], reads=[seq.b_kbT], writes=[kbT[j]])
                for j in range(NVC):
                    t0 = j * VCH
                    m = min(VCH, seq.nt - t0)
                    for u in range(0, max(m, 0), 8):
                        mm = min(8, m - u)
                        dma("sp", vbR[j][:, u:u + mm, :], seq.vb[t0 + u:t0 + u + mm].rearrange("t p n -> p t n"), reads=[seq.b_vb], writes=[vbR[j]])
                for (own, nkb) in seq.slots:
                    qb = R_q.next(); dma("sp", qb[:], qbT_d[own], reads=[b_q[own]], writes=[qb])
                    acc = ACC.next()
                    op("dve", lambda e: e.memset(acc[:], 0.0), [], [acc])
                    nst = (nkb + 1) // 2
                    for pr in range(4):
                        state = {"S": None}
                        pendB = []
                        pendC = []

                        def stA(m):
                            nb = 2 if 2 * m + 1 < nkb else 1
                            w = 256 * nb
                            kbs = [nkb - 1 - (2 * m + j) for j in range(nb)]
                            z = ZB.next()
                            for j, kb in enumerate(kbs):
                                op("pe", lambda e: e.matmul(z[:, j * 256:(j + 1) * 256], lhsT=kbT[pr][:, kb * 128:(kb + 1) * 128], rhs=qb[:, pr, :], start=True, stop=True),
                                   [kbT[pr], qb], [z])
                            ee = R_e.next()
                            op("act", lambda e: e.activation(out=ee[:, :w], in_=z[:, :w], func=AF.Exp, scale=0.125), [z], [ee])
                            if m == 0:
                                mk = sbm[:, seq.mi * 2:seq.mi * 2 + 2, :].rearrange("p a n -> p (a n)")
                                op("dve", lambda e: e.tensor_tensor(out=ee[:, :w], in0=ee[:, :w], in1=mk[:, :w], op=ALU.mult), [ee, sbm], [ee])
                            sp = R_sp.next()
                            op("act", lambda e: e.activation(out=sp[:, :w], in_=ee[:, :w], func=AF.Ln, bias=1.0), [ee], [sp])
                            Scur = state["S"]
                            if Scur is not None and nb == 2:
                                op("pool", lambda e: e.tensor_tensor(out=Scur[:, 256:512], in0=Scur[:, 0:256], in1=sp[:, 0:256], op=ALU.add), [Scur, sp], [Scur])
                            if m < nst - 1:
                                Sn = R_S.next()
                                if Scur is None:
                                    op("pool", lambda e: e.tensor_tensor(out=Sn[:, 0:256], in0=sp[:, 0:256], in1=sp[:, 256:512], op=ALU.add), [sp], [Sn])
                                else:
                                    op("pool", lambda e: e.tensor_tensor(out=Sn[:, 0:256], in0=Scur[:, 256:512], in1=sp[:, 256:512], op=ALU.add), [Scur, sp], [Sn])
                                state["S"] = Sn
                            return (m, nb, w, kbs, ee, sp, Scur, None)

                        def stB(m, nb, w, kbs, ee, sp, Scur, _unused):
                            x = XB.next()
                            op("pe", lambda e: e.matmul(x[:, :w], lhsT=tri[:], rhs=sp[:, :w], start=True, stop=False, skip_group_check=True), [tri, sp], [x])
                            if Scur is not None:
                                op("pe", lambda e: e.matmul(x[:, :w], lhsT=ones[:], rhs=Scur[:, :w], start=False, stop=True, skip_group_check=True), [ones, Scur], [x])
                            elif nb == 2:
                                op("pe", lambda e: e.matmul(x[:, 256:512], lhsT=ones[:], rhs=sp[:, 0:256], start=False, stop=True, skip_group_check=True), [ones, sp], [x])
                            f = R_f.next()
                            op("act", lambda e: e.activation(out=f[:, :w], in_=x[:, :w], func=AF.Exp, scale=-1.0), [x], [f])
                            a = R_a.next()
                            op("dve", lambda e: e.tensor_tensor(out=a[:, :w], in0=ee[:, :w], in1=f[:, :w], op=ALU.mult), [ee, f], [a])
                            return (kbs, a)

                        def stC(kbs, a):
                            for j, kb in enumerate(kbs):
                                vb_ = vbR[kb // VCH]
                                kk = kb % VCH
                                for hh in range(2):
                                    h = 2 * pr + hh
                                    c0 = j * 256 + hh * 128
                                    op("pe", lambda e: e.matmul(acc[:, h * 64:(h + 1) * 64], lhsT=a[:, c0:c0 + 128], rhs=vb_[:, kk, h * 64:(h + 1) * 64],
                                                                start=False, stop=False, skip_group_check=True), [a, vb_], [acc])

                        for m in range(nst):
                            pendB.append(stA(m))
                            if len(pendB) > 1:
                                pendC.append(stB(*pendB.pop(0)))
                            if len(pendC) > 1:
                                stC(*pendC.pop(0))
                        while pendB:
                            pendC.append(stB(*pendB.pop(0)))
                            if len(pendC) > 1:
                                stC(*pendC.pop(0))
                        while pendC:
                            stC(*pendC.pop(0))
                    ob = R_ob.next()
                    op("act", lambda e: e.copy(out=ob[:], in_=acc[:]), [acc], [ob])
                    pt = PT.next()
                    for j in range(4):
                        op("pe", lambda e: e.transpose(out=pt[:, j, :], in_=ob[:, j * 128:(j + 1) * 128], identity=ident[:]), [ob, ident], [pt])
                    oT = R_oT.next()
                    op("act", lambda e: e.copy(out=oT[:], in_=pt[:, 0:4, :]), [pt], [oT])
                    dma("pool", obT_d[own], oT[:], reads=[oT], writes=[b_ob[own]])
            fw.barrier()

        with ExitStack() as pes:
            wg = fw.sb("wg", [128, 8, 2048], BF16, pes)
            wba = fw.sb("wba", [128, 4, D], BF16, pes); wbb = fw.sb("wbb", [128, 4, D], BF16, pes)
            wo = fw.sb("wo", [128, 8, D], BF16, pes)
            stage = Rot(fw, "stgE", [128, 1024], F32, 2, pes)
            load_w(wg, w_in[:, C_GA:DIN], 8, 2048, gmix, stage)
            load_w(wba, w_ba, 4, D, None, stage); load_w(wbb, w_bb, 4, D, None, stage); load_w(wo, w_out, 8, D, None, stage)
            R_xs = Rot(fw, "xsE", [128, D], F32, 2, pes)
            R_xb = Rot(fw, "xbE", [128, D], BF16, 2, pes)
            R_xT = Rot(fw, "xTE", [128, 8, 128], BF16, 2, pes)
            junk = fw.sb("junkE", [128, D], BF16, pes)
            R_ssq = Rot(fw, "ssqE", [128, 1], F32, 2, pes)
            R_rstd = Rot(fw, "rstdE", [128, 1], F32, 3, pes)
            R_sg = Rot(fw, "sg", [128, 2048], F32, 2, pes)
            R_oaT = Rot(fw, "oaTE", [128, 4, 128], BF16, 2, pes); R_obT = Rot(fw, "obTE", [128, 4, 128], BF16, 2, pes)
            R_t1 = Rot(fw, "t1E", [128, D], F32, 2, pes); R_t2 = Rot(fw, "t2E", [128, D], F32, 2, pes)
            R_m = Rot(fw, "mE", [128, D], BF16, 2, pes)
            R_mT = Rot(fw, "mTE", [128, 8, 128], BF16, 2, pes)
            R_h = Rot(fw, "hE", [128, D], F32, 2, pes)
            R_P = RR(PB)
            for own in range(NOWN):
                xs = R_xs.next(); dma("sp", xs[:], x_own[own], writes=[xs])
                ssq = R_ssq.next()
                op("act", lambda e: e.activation(out=junk[:], in_=xs[:], func=AF.Square, accum_out=ssq[:]), [xs], [junk, ssq])
                rstd = R_rstd.next()
                op("act", lambda e: e.activation(out=rstd[:], in_=ssq[:], func=AF.Sqrt, scale=1.0 / D, bias=EPS), [ssq], [rstd])
                op("dve", lambda e: e.reciprocal(out=rstd[:], in_=rstd[:]), [rstd], [rstd])
                xb = R_xb.next()
                op("dve", lambda e: e.tensor_copy(out=xb[:], in_=xs[:]), [xs], [xb])
                pt = PT.next()
                for c in range(8):
                    op("pe", lambda e: e.transpose(out=pt[:, c, :], in_=xb[:, c * 128:(c + 1) * 128], identity=ident[:]), [xb, ident], [pt])
                xT = R_xT.next()
                op("act", lambda e: e.copy(out=xT[:], in_=pt[:]), [pt], [xT])
                sg = R_sg.next()
                for g2 in range(0, 4, 2):
                    Pg = [R_P.next(), R_P.next()]
                    for c in range(8):
                        for k_, P in enumerate(Pg):
                            g = g2 + k_
                            op("pe", lambda e: e.matmul(P[:], lhsT=xT[:, c, :], rhs=wg[:, c, g * 512:(g + 1) * 512], start=(c == 0), stop=(c == 7)), [xT, wg], [P])
                    for k_, P in enumerate(Pg):
                        g = g2 + k_
                        op("act", lambda e: e.activation(out=sg[:, g * 512:(g + 1) * 512], in_=P[:], func=AF.Sigmoid, scale=rstd[:, 0:1]), [P, rstd], [sg])
                oaT = R_oaT.next(); dma("sp", oaT[:], oaT_d[own], reads=[b_oa[own]], writes=[oaT])
                obT = R_obT.next(); dma("sp", obT[:], obT_d[own], reads=[b_ob[own]], writes=[obT])
                t1 = R_t1.next(); t2 = R_t2.next()
                for (oT, wb_, tt, off, eng) in ((oaT, wba, t1, 0, "dve"), (obT, wbb, t2, 1024, "pool")):
                    Pg = [R_P.next(), R_P.next()]
                    for c in range(4):
                        for g, P in enumerate(Pg):
                            op("pe", lambda e: e.matmul(P[:], lhsT=oT[:, c, :], rhs=wb_[:, c, g * 512:(g + 1) * 512], start=(c == 0), stop=(c == 3)), [oT, wb_], [P])
                    for g, P in enumerate(Pg):
                        op("dve", lambda e: e.tensor_tensor(out=tt[:, g * 512:(g + 1) * 512], in0=P[:], in1=sg[:, off + g * 512:off + (g + 1) * 512], op=ALU.mult),
                           [P, sg], [tt])
                m = R_m.next()
                op("pool", lambda e: e.tensor_tensor(out=m[:], in0=t1[:], in1=t2[:], op=ALU.add), [t1, t2], [m])
                pt = PT.next()
                for c in range(8):
                    op("pe", lambda e: e.transpose(out=pt[:, c, :], in_=m[:, c * 128:(c + 1) * 128], identity=ident[:]), [m, ident], [pt])
                mT = R_mT.next()
                op("act", lambda e: e.copy(out=mT[:], in_=pt[:]), [pt], [mT])
                hh = R_h.next()
                Pg = [R_P.next(), R_P.next()]
                for c in range(8):
                    for g, P in enumerate(Pg):
                        op("pe", lambda e: e.matmul(P[:], lhsT=mT[:, c, :], rhs=wo[:, c, g * 512:(g + 1) * 512], start=(c == 0), stop=(c == 7)), [mT, wo], [P])
                for g, P in enumerate(Pg):
                    op("dve", lambda e: e.tensor_tensor(out=hh[:, g * 512:(g + 1) * 512], in0=P[:], in1=xs[:, g * 512:(g + 1) * 512], op=ALU.add), [P, xs], [hh])
                dma("pool", h_d[own], hh[:], reads=[hh], writes=[b_h[own]])
            fw.barrier()

        with ExitStack() as pes:
            wu = fw.sb("wu", [128, 8, DFF], BF16, pes)
            wd = fw.sb("wd", [128, 32, D], BF16, pes)
            stage = Rot(fw, "stgF", [128, 1024], F32, 2, pes)
            load_w(wu, w_up, 8, DFF, gffn, stage)
            load_w(wd, w_down, 32, D, None, stage)
            GT = 2
            R_h = Rot(fw, "hF", [128, D], F32, 4, pes)
            R_hb = Rot(fw, "hbF", [128, D], BF16, 2, pes)
            R_hT = Rot(fw, "hTF", [128, 8, GT * 128], BF16, 2, pes)
            junk = fw.sb("junkF", [128, D], BF16, pes)
            R_ssq = Rot(fw, "ssqF", [128, 1], F32, 4, pes)
            R_r2 = Rot(fw, "r2F", [128, 1], F32, 6, pes)
            R_r1 = Rot(fw, "r1F", [128, GT * 128], BF16, 3, pes)
            R_rT = Rot(fw, "rTF", [128, 32, GT * 128], BF16, 1, pes)
            R_y = Rot(fw, "yF", [128, D], F32, 2, pes)
            R_P = RR(PB)
            for g0 in range(0, NOWN, GT):
                tiles = list(range(g0, min(NOWN, g0 + GT)))
                nt_ = len(tiles)
                hT = R_hT.next()
                hs = []
                r2s = []
                for j, own in enumerate(tiles):
                    h = R_h.next(); dma("sp", h[:], h_d[own], reads=[b_h[own]], writes=[h])
                    hs.append(h)
                    ssq = R_ssq.next()
                    op("act", lambda e: e.activation(out=junk[:], in_=h[:], func=AF.Square, accum_out=ssq[:]), [h], [junk, ssq])
                    r2 = R_r2.next()
                    op("dve", lambda e: e.tensor_scalar(out=r2[:], in0=ssq[:], scalar1=1.0 / D, scalar2=EPS, op0=ALU.mult, op1=ALU.add), [ssq], [r2])
                    op("dve", lambda e: e.reciprocal(out=r2[:], in_=r2[:]), [r2], [r2])
                    r2s.append(r2)
                    hb = R_hb.next()
                    op("dve", lambda e: e.tensor_copy(out=hb[:], in_=h[:]), [h], [hb])
                    pt = PT.next()
                    for c in range(8):
                        op("pe", lambda e: e.transpose(out=pt[:, c, :], in_=hb[:, c * 128:(c + 1) * 128], identity=ident[:]), [hb, ident], [pt])
                    op("act", lambda e: e.copy(out=hT[:, :, j * 128:(j + 1) * 128], in_=pt[:]), [pt], [hT])
                NN = nt_ * 128
                rT = R_rT.next()
                for f0 in range(0, 32, 2):
                    Pg = [R_P.next(), R_P.next()]
                    for c in range(8):
                        for k_, P in enumerate(Pg):
                            f = f0 + k_
                            op("pe", lambda e: e.matmul(P[:, :NN], lhsT=wu[:, c, f * 128:(f + 1) * 128], rhs=hT[:, c, :NN], start=(c == 0), stop=(c == 7)), [wu, hT], [P])
                    for k_, P in enumerate(Pg):
                        f = f0 + k_
                        r1 = R_r1.next()
                        op("act", lambda e: e.activation(out=r1[:, :NN], in_=P[:, :NN], func=AF.Relu), [P], [r1])
                        op("pool", lambda e: e.tensor_tensor(out=rT[:, f, :NN], in0=r1[:, :NN], in1=r1[:, :NN], op=ALU.mult), [r1], [rT])
                for j, own in enumerate(tiles):
                    y = R_y.next()
                    Pg = [R_P.next(), R_P.next()]
                    for f in range(32):
                        for g, P in enumerate(Pg):
                            op("pe", lambda e: e.matmul(P[:], lhsT=rT[:, f, j * 128:(j + 1) * 128], rhs=wd[:, f, g * 512:(g + 1) * 512], start=(f == 0), stop=(f == 31)),
                               [rT, wd], [P])
                    for g, P in enumerate(Pg):
                        op("dve", lambda e: e.scalar_tensor_tensor(out=y[:, g * 512:(g + 1) * 512], in0=P[:], scalar=r2s[j][:, 0:1], in1=hs[j][:, g * 512:(g + 1) * 512],
                                                                   op0=ALU.mult, op1=ALU.add), [P, r2s[j], hs[j]], [y])
                    dma("pool", y_own[own], y[:], reads=[y])
            fw.barrier()
        fw.finish("sp")
        print("bass program built: %d instructions" % fw.ninst, fw.per, "nsems", len(fw.sems))
    return nc


def _rope_tab(pos):
    half = 32
    inv = np.power(np.float32(10000.0), -np.arange(half, dtype=np.float32) / np.float32(half)).astype(np.float32)
    ang = pos.astype(np.float32)[:, None] * inv[None, :]
    return np.concatenate([np.cos(ang), np.sin(ang)], axis=1).astype(np.float32)


def _consts(par, S, PAST, T):
    ident = np.eye(128, dtype=np.float32)
    ident2 = np.concatenate([ident, ident], 1)
    j = np.arange(128)[:, None]; s = np.arange(128)[None, :]
    tri = (j >= s).astype(np.float32)
    ones = np.ones((128, 128), np.float32)
    dm = np.zeros((2, 128, 256), np.float32)
    q = np.arange(128)[:, None]; k = np.arange(256)[None, :]
    qch = (par * 128 + q) // 64
    kch = k // 64
    dm[0] = np.where(kch <= qch, 0.0, MASKED)
    dm[1] = np.where(k < 128 + T, 0.0, MASKED) * np.ones((128, 1), np.float32)
    sb = np.zeros((2, 2, 128, 128), np.float32)
    kk = np.arange(128)[:, None]; qq = np.arange(128)[None, :]
    for n in range(2):
        kb_rel = 1 - n
        sb[0, n] = ((kb_rel * 128 + kk) < (par * 128 + qq)).astype(np.float32)
    sb[1, 0] = ((kk < qq) & (kk < T)).astype(np.float32)
    sb[1, 1] = 1.0
    sb2 = np.concatenate([sb, sb], axis=3)
    return dict(c_ident=ident.astype(NPBF), c_ident2=ident2.astype(NPBF), c_tri=tri.astype(NPBF), c_ones=ones.astype(NPBF),
                c_dmask=dm, c_sbm=sb2.astype(NPBF))


_PROG = {}


def run_all(inp, S, PAST, T, KP, KS):
    key = (S, PAST, KP, KS)
    if key not in _PROG:
        _PROG[key] = build_program(S, PAST, KP, KS)
    nc = _PROG[key]
    NTP = S // 128; NSP = NTP // 2; NTC = PAST // 128; NOWN = NSP + 2
    f = lambda a: np.ascontiguousarray(np.asarray(a, dtype=np.float32))
    xp = f(inp["x_prompt"]); xsm = f(inp["x_sample"])
    caches = {k: f(inp[k]) for k in ("cache_k_a", "cache_v_a", "cache_k_idx", "cache_k_sb", "cache_v_sb")}
    rope_p = _rope_tab(np.arange(S)).reshape(NTP, 128, 64)
    rs = np.zeros((128, 64), np.float32); rs[:, :32] = 1.0
    rs[:T] = _rope_tab(PAST + np.arange(T))
    in_maps = []
    for c in range(8):
        b, par = c // 2, c % 2
        m = {}
        xa = xp[b].reshape(NTP, 128, D)
        m["x_all"] = xa
        xo = np.zeros((NOWN, 128, D), np.float32)
        xo[:NSP] = xa[par::2]
        ro = np.zeros((NOWN, 128, 64), np.float32)
        ro[:NSP] = rope_p[par::2]
        for s in range(2):
            xo[NSP + s, :T] = xsm[2 * c + s]
            ro[NSP + s] = rs
        m["x_own"] = xo; m["rope_all"] = rope_p; m["rope_own"] = ro
        sl = slice(2 * c, 2 * c + 2)
        m["ck_a"] = caches["cache_k_a"][sl].reshape(2, NTC, 128, W); m["cv_a"] = caches["cache_v_a"][sl].reshape(2, NTC, 128, W)
        m["ck_i"] = caches["cache_k_idx"][sl].reshape(2, NTC, 128, 64)
        m["ck_b"] = caches["cache_k_sb"][sl].reshape(2, NTC, 128, W); m["cv_b"] = caches["cache_v_sb"][sl].reshape(2, NTC, 128, W)
        m["w_in"] = f(inp["w_in"]); m["g_mix"] = np.ascontiguousarray(f(inp["g_mix"]).reshape(8, 128).T)
        m["g_qn"] = f(inp["g_qn"]); m["g_kn"] = f(inp["g_kn"])
        m["w_ba"] = f(inp["w_branch_a"]); m["w_bb"] = f(inp["w_branch_b"]); m["w_out"] = f(inp["w_out"])
        m["g_ffn"] = np.ascontiguousarray(f(inp["g_ffn"]).reshape(8, 128).T)
        m["w_up"] = f(inp["w_up"]); m["w_down"] = f(inp["w_down"])
        m.update(_consts(par, S, PAST, T))
        in_maps.append(m)
    res = run_bass_kernel_spmd(nc, in_maps, core_ids=list(range(8)))
    R = res.results
    LAST["R"] = R
    B = 4; DB = 16
    y_p = np.zeros((B, NTP, 128, D), np.float32)
    y_s = np.zeros((DB, T, D), np.float32)
    outs_p = {k: np.zeros((B, NTP, 128, n), np.float32) for k, n in (("o_ka", W), ("o_va", W), ("o_ki", 64), ("o_kb", W), ("o_vb", W))}
    outs_s = {k: np.zeros((DB, T, n), np.float32) for k, n in (("s_ka", W), ("s_va", W), ("s_ki", 64), ("s_kb", W), ("s_vb", W))}
    for c in range(8):
        b, par = c // 2, c % 2
        r = R[c]
        y_p[b, par::2] = r["y_own"][:NSP]
        for s in range(2):
            y_s[2 * c + s] = r["y_own"][NSP + s, :T]
            for k in outs_s:
                outs_s[k][2 * c + s] = r[k][s, :T]
        for k in outs_p:
            outs_p[k][b, par::2] = r[k][par::2]
    return (y_p.reshape(B, S, D), y_s,
            outs_p["o_ka"].reshape(B, S, NH, HD), outs_p["o_va"].reshape(B, S, NH, HD), outs_p["o_ki"].reshape(B, S, 64),
            outs_p["o_kb"].reshape(B, S, NH, HD), outs_p["o_vb"].reshape(B, S, NH, HD),
            outs_s["s_ka"].reshape(DB, T, NH, HD), outs_s["s_va"].reshape(DB, T, NH, HD), outs_s["s_ki"],
            outs_s["s_kb"].reshape(DB, T, NH, HD), outs_s["s_vb"].reshape(DB, T, NH, HD))


def kernel(**inputs):
    S = inputs["x_prompt"].shape[1]
    PAST = inputs["cache_k_a"].shape[1]
    T = inputs["x_sample"].shape[1]
    return run_all(inputs, S, PAST, T, min(256, S // 4), min(256, (PAST + T) // 4))
```
